# Optimizing a Trainium2 kernel written in Bass

```python
import jax, jax.numpy as jnp
from jax import lax
import numpy as np

D_MODEL = 1024
BATCH = 16
SEQ = 2048
DEPTH = 2

HEAD_DIM = 64
N_FOX_HEADS = 6
N_RET_HEADS = 6
N_SB_HEADS = 4
N_HEADS = N_FOX_HEADS + N_RET_HEADS + N_SB_HEADS
D_MIX = N_HEADS * HEAD_DIM
D_FOX = N_FOX_HEADS * HEAD_DIM
D_RET = N_RET_HEADS * HEAD_DIM
D_SB = N_SB_HEADS * HEAD_DIM
D_IN = 4 * D_MIX + N_FOX_HEADS
Q_BLOCK = 128
RET_CHUNK = 128
ROPE_BASE = 10000.0
LN_EPS = 1e-5
GN_EPS = 1e-5
DEEPNORM_ALPHA = (2 * DEPTH) ** 0.25
DEEPNORM_BETA = (8 * DEPTH) ** -0.25
FGATE_BIAS_MEAN = 3.0

kernel_name = 'hybrid_fox_retnet_stickbreaking'


def _layernorm(x, gain, bias):
    xf = x.astype(jnp.float32)
    mu = jnp.mean(xf, axis=-1, keepdims=True)
    var = jnp.mean(jnp.square(xf - mu), axis=-1, keepdims=True)
    return ((xf - mu) * lax.rsqrt(var + LN_EPS) * gain + bias).astype(x.dtype)


def _rotary(t, pos):
    half = t.shape[-1] // 2
    inv_freq = 1.0 / (ROPE_BASE ** (jnp.arange(half, dtype=jnp.float32) / half))
    ang = pos[:, None] * inv_freq[None, :]
    cos, sin = jnp.cos(ang), jnp.sin(ang)
    t1, t2 = t[..., :half], t[..., half:]
    return jnp.concatenate([t1 * cos - t2 * sin, t1 * sin + t2 * cos], axis=-1)


def _to_blocks(t, n_blocks, block):
    B, H = t.shape[0], t.shape[1]
    t = t.reshape((B, H, n_blocks, block) + t.shape[3:])
    return jnp.moveaxis(t, 2, 0)


def _from_blocks(o):
    nb, B, H, blk, d = o.shape
    return o.transpose(1, 0, 3, 2, 4).reshape(B, nb * blk, H * d)


def _fox_attention(q, k, v, c):
    B, H, S, d = q.shape
    nb = S // Q_BLOCK
    scale = d ** -0.5
    k_pos = jnp.arange(S)

    def block(args):
        qi, ci, i = args
        q_pos = i * Q_BLOCK + jnp.arange(Q_BLOCK)
        s = jnp.einsum('bhqd,bhkd->bhqk', qi, k).astype(jnp.float32) * scale
        s = s + ci[..., None] - c[:, :, None, :]
        s = jnp.where(k_pos[None, :] <= q_pos[:, None], s, -jnp.inf)
        p = jax.nn.softmax(s, axis=-1)
        return jnp.einsum('bhqk,bhkd->bhqd', p.astype(v.dtype), v)

    o = lax.map(block, (_to_blocks(q, nb, Q_BLOCK), _to_blocks(c, nb, Q_BLOCK), jnp.arange(nb)))
    return _from_blocks(o)


def _stick_breaking(q, k, v):
    B, H, S, d = q.shape
    nb = S // Q_BLOCK
    scale = d ** -0.5
    k_pos = jnp.arange(S)

    def block(args):
        qi, i = args
        q_pos = i * Q_BLOCK + jnp.arange(Q_BLOCK)
        z = jnp.einsum('bhqd,bhkd->bhqk', qi, k).astype(jnp.float32) * scale
        strict = k_pos[None, :] < q_pos[:, None]
        log_rem = jnp.where(strict, jax.nn.log_sigmoid(-z), 0.0)
        tail = lax.cumsum(log_rem, axis=3, reverse=True) - log_rem
        log_a = jax.nn.log_sigmoid(z) + tail
        a = jnp.where(strict, jnp.exp(log_a), 0.0)
        return jnp.einsum('bhqk,bhkd->bhqd', a.astype(v.dtype), v)

    o = lax.map(block, (_to_blocks(q, nb, Q_BLOCK), jnp.arange(nb)))
    return _from_blocks(o)


def _retention(q, k, v, gn_gain):
    B, H, S, d = q.shape
    C = RET_CHUNK
    nc = S // C
    pos = jnp.arange(S, dtype=jnp.float32)
    q = _rotary(q, pos)
    k = _rotary(k, pos) * (d ** -0.5)
    log_g = jnp.log(1.0 - 2.0 ** (-5.0 - jnp.arange(H, dtype=jnp.float32)))
    idx = jnp.arange(C, dtype=jnp.float32)
    diff = idx[:, None] - idx[None, :]
    intra_decay = jnp.where(diff >= 0, jnp.exp(jnp.maximum(diff, 0.0)[None] * log_g[:, None, None]), 0.0)
    query_decay = jnp.exp((idx[None, :] + 1.0) * log_g[:, None])
    key_decay = jnp.exp((C - 1.0 - idx[None, :]) * log_g[:, None])
    chunk_decay = jnp.exp(C * log_g)

    def step(state, inp):
        qc, kc, vc = inp
        scores = jnp.einsum('bhnd,bhmd->bhnm', qc, kc) * intra_decay
        o = jnp.einsum('bhnm,bhmv->bhnv', scores, vc)
        o = o + jnp.einsum('bhnd,bhdv->bhnv', qc, state) * query_decay[:, :, None]
        new_state = chunk_decay[:, None, None] * state + jnp.einsum(
            'bhmd,bhmv->bhdv', kc * key_decay[:, :, None], vc)
        return new_state, o

    state0 = jnp.zeros((B, H, d, d), jnp.float32)
    _, o = lax.scan(step, state0, (_to_blocks(q, nc, C), _to_blocks(k, nc, C), _to_blocks(v, nc, C)))
    o = o.transpose(1, 0, 3, 2, 4)
    o = o.reshape(B, S, H, d)
    mu = jnp.mean(o, axis=-1, keepdims=True)
    var = jnp.mean(jnp.square(o - mu), axis=-1, keepdims=True)
    o = (o - mu) * lax.rsqrt(var + GN_EPS) * gn_gain.astype(jnp.float32).reshape(H, d)
    return o.reshape(B, S, H * d)


def _hybrid_layer(x, w_in, b_fgate, gn_gain, w_out, ln_gain, ln_bias):
    B, S, _ = x.shape
    proj = jnp.einsum('bsd,de->bse', x, w_in)
    q = proj[..., :D_MIX]
    k = proj[..., D_MIX:2 * D_MIX]
    v = proj[..., 2 * D_MIX:3 * D_MIX]
    gate = proj[..., 3 * D_MIX:4 * D_MIX]
    f_logit = (proj[..., 4 * D_MIX:] + b_fgate).astype(jnp.float32)

    def heads(t):
        return t.reshape(B, S, N_HEADS, HEAD_DIM).transpose(0, 2, 1, 3)

    q, k, v = heads(q), heads(k), heads(v)
    f0, f1 = 0, N_FOX_HEADS
    r0, r1 = N_FOX_HEADS, N_FOX_HEADS + N_RET_HEADS

    c = lax.cumsum(jax.nn.log_sigmoid(f_logit), axis=1).transpose(0, 2, 1)
    o_fox = _fox_attention(q[:, f0:f1], k[:, f0:f1], v[:, f0:f1], c)
    o_ret = _retention(q[:, r0:r1].astype(jnp.float32), k[:, r0:r1].astype(jnp.float32),
                       v[:, r0:r1].astype(jnp.float32), gn_gain).astype(x.dtype)
    o_sb = _stick_breaking(q[:, r1:], k[:, r1:], v[:, r1:])

    y = jnp.concatenate([o_fox.astype(x.dtype), o_ret, o_sb.astype(x.dtype)], axis=-1) * jax.nn.silu(gate)
    out = jnp.einsum('bse,ed->bsd', y, w_out)
    return _layernorm(DEEPNORM_ALPHA * x + out, ln_gain, ln_bias)


def setup_inputs(seed: int = 0) -> dict:
    key = jax.random.key(seed)
    ks = jax.random.split(key, 10)
    std_in = D_MODEL ** -0.5
    x = jax.random.normal(ks[0], (BATCH, SEQ, D_MODEL), jnp.float32)
    w_qk = jax.random.normal(ks[1], (DEPTH, D_MODEL, 2 * D_MIX), jnp.float32) * std_in
    w_v = jax.random.normal(ks[2], (DEPTH, D_MODEL, D_MIX), jnp.float32) * (std_in * DEEPNORM_BETA)
    w_g = jax.random.normal(ks[3], (DEPTH, D_MODEL, D_MIX), jnp.float32) * std_in
    w_f = jax.random.normal(ks[4], (DEPTH, D_MODEL, N_FOX_HEADS), jnp.float32) * std_in
    w_in = jnp.concatenate([w_qk, w_v, w_g, w_f], axis=-1)
    b_fgate = FGATE_BIAS_MEAN + jax.random.normal(ks[5], (DEPTH, N_FOX_HEADS), jnp.float32)
    ret_gn_gain = 1.0 + 0.02 * jax.random.normal(ks[6], (DEPTH, D_RET), jnp.float32)
    w_out = jax.random.normal(ks[7], (DEPTH, D_MIX, D_MODEL), jnp.float32) * (D_MIX ** -0.5 * DEEPNORM_BETA)
    ln_gain = 1.0 + 0.02 * jax.random.normal(ks[8], (DEPTH, D_MODEL), jnp.float32)
    ln_bias = 0.02 * jax.random.normal(ks[9], (DEPTH, D_MODEL), jnp.float32)
    return {'x': x, 'w_in': w_in, 'b_fgate': b_fgate, 'ret_gn_gain': ret_gn_gain,
            'w_out': w_out, 'ln_gain': ln_gain, 'ln_bias': ln_bias}


def reference(x, w_in, b_fgate, ret_gn_gain, w_out, ln_gain, ln_bias):
    for layer in range(DEPTH):
        x = _hybrid_layer(x, w_in[layer], b_fgate[layer], ret_gn_gain[layer],
                          w_out[layer], ln_gain[layer], ln_bias[layer])
    return x
```

```python
import contextlib
import math
import numpy as np
import concourse.bass as bass
import concourse.mybir as mybir
from concourse.bass_utils import run_bass_kernel_spmd

F32 = mybir.dt.float32
BF16 = mybir.dt.bfloat16
AF = mybir.ActivationFunctionType
ALU = mybir.AluOpType

S = 2048
D = 1024
NB = 16
NT = 4
NKC = 8
DEPTH = 2
ALPHA = (2 * DEPTH) ** 0.25
LN_EPS = 1e-5
GN_EPS = 1e-5
NEG = -30000.0
RET_DEPTH = 2
RET_FILL = True
SB_DUMMY = 2
INTERLEAVE = False

ENGS = ("pe", "act", "dve", "pool", "sp")
N_DMA_SEMS = 6


class Res:
    __slots__ = ("w", "r")

    def __init__(self):
        self.w = None
        self.r = {}


class Prog:
    def __init__(self, nc):
        self.nc = nc
        self.streams = {e: [] for e in ENGS}
        self.count = {e: 0 for e in ENGS}
        self.waited = {e: {} for e in ENGS}
        self.dma_cnt = {}
        self.dma_rr = {e: 0 for e in ENGS}

    def _wait(self, eng, dep):
        key, val = dep
        if eng == "pe" and key == "pe":
            return
        if self.waited[eng].get(key, 0) >= val:
            return
        self.waited[eng][key] = val
        self.streams[eng].append(("wait", key, val))

    def _deps(self, eng, reads, writes):
        for r in reads:
            if r.w is not None:
                self._wait(eng, r.w)
        for w in writes:
            if w.w is not None:
                self._wait(eng, w.w)
            for k, v in w.r.items():
                self._wait(eng, (k, v))

    def _mark(self, ev, reads, writes):
        k, v = ev
        for r in reads:
            if r.r.get(k, 0) < v:
                r.r[k] = v
        for w in writes:
            w.w = ev
            w.r = {}

    def op(self, eng, emit, reads=(), writes=()):
        self._deps(eng, reads, writes)
        self.count[eng] += 1
        idx = self.count[eng]
        self.streams[eng].append(("op", emit, idx))
        self._mark((eng, idx), reads, writes)

    def dma(self, q, out, in_, reads=(), writes=()):
        self._deps(q, reads, writes)
        k = self.dma_rr[q]
        self.dma_rr[q] = (k + 1) % N_DMA_SEMS
        key = "dma_%s_%d" % (q, k)
        prev = self.dma_cnt.get(key, 0)
        if prev:
            self._wait(q, (key, prev))
        val = prev + 16
        self.dma_cnt[key] = val
        self.streams[q].append(("dma", out, in_, key))
        self._mark((key, val), reads, writes)

    def handoff(self, src, dst):
        acc = {}
        for r in src:
            if r.w is not None:
                acc[r.w[0]] = max(acc.get(r.w[0], 0), r.w[1])
            for k, v in r.r.items():
                acc[k] = max(acc.get(k, 0), v)
        for d in dst:
            for k, v in acc.items():
                if d.r.get(k, 0) < v:
                    d.r[k] = v

    def final_wait_all(self, eng="sp"):
        for key, val in self.dma_cnt.items():
            self._wait(eng, (key, val))

    def emit(self):
        nc = self.nc
        keys = [e for e in ENGS if self.count[e] > 0] + sorted(self.dma_cnt.keys())
        with contextlib.ExitStack() as st:
            sems = {k: st.enter_context(nc.semaphore("s_" + k)) for k in keys}
            block = st.enter_context(nc.Block())

            def run(eng_name):
                def body(e):
                    for item in self.streams[eng_name]:
                        if item[0] == "wait":
                            e.wait_ge(sems[item[1]], item[2])
                        elif item[0] == "op":
                            item[1](e).then_inc(sems[eng_name], 1)
                        else:
                            _, out, in_, key = item
                            e.dma_start(out=out, in_=in_).then_inc(sems[key], 16)
                return body

            block.sync(run("sp"))
            block.tensor(run("pe"))
            block.scalar(run("act"))
            block.vector(run("dve"))
            block.gpsimd(run("pool"))


class RR:
    def __init__(self, items):
        self.items = items
        self.i = 0

    def get(self):
        it = self.items[self.i]
        self.i = (self.i + 1) % len(self.items)
        return it


def build(n_seq=2, n_layers=2, dbg_stop=None, dbg_y=False):
    nc = bass.Bass("TRN2", target_bir_lowering=False)
    P = Prog(nc)

    def din(name, shape):
        return nc.dram_tensor(name, shape, F32, kind="ExternalInput").ap()

    x_h = din("x", [n_seq, S, D])
    wp_h = din("wp", [DEPTH, 8, 128, NKC * 512])
    wf_h = din("wf", [DEPTH, 128, NKC * 6])
    wo_h = din("wo", [DEPTH, 2, 128, NKC * 512])
    bf_h = din("bfg", [DEPTH, 6, 1])
    gn_h = din("gng", [DEPTH, 1, 384])
    lg_h = din("lng", [DEPTH, 1, D])
    lb_h = din("lnb", [DEPTH, 1, D])
    c128_h = din("c128", [128, 8 * 128])
    oh_h = din("oh", [128, 32])
    msel_h = din("msel", [16, S])
    cos_h = din("cost", [128, S])
    sin_h = din("sint", [128, S])
    iota_h = din("iota", [128, 512])
    dec_h = din("dec", [128, 3 * 10])
    out_h = nc.dram_tensor("out", [n_seq, S, D], F32, kind="ExternalOutput").ap()
    if dbg_y:
        dbgy_h = nc.dram_tensor("dbgy", [128, NKC * S], BF16, kind="ExternalOutput").ap()

    with contextlib.ExitStack() as st:
        def sb(name, shape, dt=BF16):
            return st.enter_context(nc.sbuf_tensor(name, shape, dt))

        def psum(name):
            return st.enter_context(nc.psum_tensor(name, [128, 512], F32))

        xres = sb("xres", [128, NB, D], F32)
        Rxres = [Res() for _ in range(NB)]
        xT = sb("xT", [128, NKC, S])
        RxT = [Res() for _ in range(NT)]
        yT = sb("yT", [128, NKC, S])
        RyT = [[Res() for _ in range(NT)] for _ in range(8)]
        wbuf = [sb("wbuf%d" % i, [128, NKC, 512]) for i in range(2)]
        Rw = [Res() for _ in range(2)]
        wfb2 = sb("wfb", [128, DEPTH, NKC, 6]); Rwf = Res()
        QA = sb("QA", [128, S]); QB = sb("QB", [128, S]); KA = sb("KA", [128, S]); KB = sb("KB", [128, S])
        RQ = [[Res() for _ in range(NT)] for _ in range(2)]
        RK = [[Res() for _ in range(NT)] for _ in range(2)]
        RQaug = [Res(), Res()]
        RKaug = [Res(), Res()]
        Qh = [QA, QB]; Kh = [KA, KB]
        VA = sb("VA", [128, NB, 128]); VB = sb("VB", [128, NB, 128])
        Vh = [VA, VB]
        RV = [[Res() for _ in range(NT)] for _ in range(2)]
        RVones = Res()
        sgA = sb("sgA", [128, S]); sgB = sb("sgB", [128, S])
        sgh = [sgA, sgB]
        Rsg = [[Res() for _ in range(NT)] for _ in range(2)]
        SPall = sb("SPall", [128, NB * 512])
        SPb = [SPall[:, i * 512:(i + 1) * 512] for i in range(NB)]
        RSP = [Res() for _ in range(NB)]
        Ktok = SPall[:, 0:2048].rearrange("p (b c) -> p b c", b=NB); RKtok = [Res() for _ in range(NT)]
        Pt = RR([(sb("Pt%d" % i, [128, 512]), Res()) for i in range(2)])
        f32t = RR([(sb("f32t%d" % i, [128, 512], F32), Res()) for i in range(3)])
        Ebuf = f32t
        rot = [SPall[:, 2048:3072].bitcast(F32), SPall[:, 3072:4096].bitcast(F32)]
        Rrot = Res()
        xb = RR([(sgA[:, 0:1024], Res()), (sgA[:, 1024:2048], Res())])
        xb_res = [xb.items[0][1], xb.items[1][1]]
        csb = RR([(sb("csb%d" % i, [16, 512]), Res()) for i in range(1)])
        negcspT = SPall[0:6, 0:2048]; Rncsp = [Res() for _ in range(NT)]
        csp_tok = sb("csp_tok", [128, NB, 6], F32); Rcsptok = [Res() for _ in range(NT)]
        cs_tmp = RR([(SPall[0:6, 4096 + i * 1024:5120 + i * 1024].bitcast(F32), Res()) for i in range(2)])
        ft1 = RR([(SPall[0:6, 6144 + i * 1024:7168 + i * 1024].bitcast(F32), Res()) for i in range(2)])
        ones6 = SPall[0:6, 2048:3072].bitcast(F32); Rones6 = Res()
        negb = sb("negb", [6, 1], F32); Rnegb = Res()
        braw = sb("braw", [6, 1], F32); Rbraw = Res()
        gng = sb("gng_sb", [128, 384], F32); Rgng = Res()
        lngain = KA[:].bitcast(F32); Rlng = Res()
        lnbias = KB[:].bitcast(F32); Rlnb = Res()
        c128 = sb("c128_sb", [128, 8 * 128]); Rc = Res()
        identf = sb("identf", [6, 8], F32)
        oh = sb("oh_sb", [128, 32])
        iota = SPall[:, 4096:5120].bitcast(F32); Riota = Res()
        dec = sb("dec_sb", [128, 30], F32)
        mask2 = sb("mask2", [128, 256])
        Sall = SPall[:, 5120:7168].rearrange("p (b c) -> p b c", b=NB)
        RSall = [Res() for _ in range(NB)]
        st8 = RR([(sb("st8_%d" % i, [128, 8, 6], F32), Res()) for i in range(2)])
        mv8 = RR([(sb("mv8_%d" % i, [128, 8, 2], F32), Res()) for i in range(2)])
        rs8 = RR([(sb("rs8_%d" % i, [128, 8], F32), Res()) for i in range(2)])
        lnst = RR([(sb("lnst%d" % i, [128, 2, 6], F32), Res()) for i in range(4)])
        lnmv = RR([(sb("lnmv%d" % i, [128, 4], F32), Res()) for i in range(4)])
        zt = RR([(QA[:].bitcast(F32), Res()), (QB[:].bitcast(F32), Res()),
                 (VA[:].rearrange("p b c -> p (b c)").bitcast(F32), Res()), (VB[:].rearrange("p b c -> p (b c)").bitcast(F32), Res())])
        qk_res = [r for grp in (RQ, RK, RV) for hh in grp for r in hh] + RQaug + RKaug + [RVones]
        ln_res = [it[1] for it in zt.items] + [Rlng, Rlnb]

        ident = c128[:, 0:128]
        m01 = c128[:, 128:256]
        m01s = c128[:, 256:384]
        negfox = c128[:, 384:512]
        negsb = c128[:, 512:640]
        neguinc = c128[:, 640:768]
        negones = c128[:, 768:896]
        zeros128 = c128[:, 896:1024]

        pbig = RR([(psum("pb%d" % i), Res()) for i in range(3)])
        pobank = RR([(psum("po%d" % i), Res()) for i in range(2)])
        pproj = RR([(psum("pp%d" % i), Res()) for i in range(2)])
        pmisc = RR([(psum("pm%d" % i), Res()) for i in range(1)])
        pln = RR(pproj.items + pbig.items + pobank.items)
        sbbig = RR(pbig.items + pmisc.items)
        ppT = RR(list(pproj.items))
        if not INTERLEAVE:
            pproj = RR(pproj.items + pbig.items)
            pvproj = RR(pmisc.items + pobank.items)
        else:
            pvproj = pmisc

        print("sbuf bytes remaining:", nc.sbuf_bytes_remaining)

        P.dma("pool", c128[:], c128_h, writes=[Rc])
        P.dma("pool", oh[:], oh_h, writes=[Rc])
        P.dma("sp", identf[:], c128_h[0:6, 0:8], writes=[Rc])
        P.dma("sp", dec[:], dec_h, writes=[Rc])
        P.dma("pool", mask2[:, 0:128], c128_h[:, 128:256], writes=[Rc])
        P.dma("pool", mask2[:, 128:256], c128_h[:, 128:256], writes=[Rc])
        fox_scr = Rncsp + [Rones6] + [it[1] for it in cs_tmp.items] + [it[1] for it in ft1.items]
        ret_scr = RKtok + [Rrot, Riota] + RSall
        sb_scr = RSP

        def ones_init(e):
            e.memset(VA[:, :, 64:128], 1.0)
            return e.memset(VB[:, :, 64:128], 1.0)
        P.op("pool", ones_init, writes=[RVones])
        for l_ in range(DEPTH):
            P.dma("pool", wfb2[:, l_, :, :].rearrange("p k c -> p (k c)"), wf_h[l_], writes=[Rwf])

        wstate = {"i": 0}

        def load_w(src_ap):
            i = wstate["i"]
            wstate["i"] = 1 - i
            for kc in range(NKC):
                P.dma("pool", wbuf[i][:, kc, :], src_ap[:, kc * 512:(kc + 1) * 512], writes=[Rw[i]])
            return wbuf[i], Rw[i]

        def make_xT(tb, evac_eng="dve"):
            P.handoff(Rsg[0] + Rsg[1], xb_res)
            xbt, Rxb = xb.get()
            P.op("act", lambda e: e.activation(out=xbt[:], in_=xres[:, tb, :], func=AF.Copy), reads=[Rxres[tb]], writes=[Rxb])
            pm, Rpm = pmisc.get()
            pmb = pm[:].bitcast(BF16)

            def tr(e):
                ins = None
                for kc in range(NKC):
                    ins = e.transpose(out=pmb[:, kc * 128:(kc + 1) * 128], in_=xbt[:, kc * 128:(kc + 1) * 128], identity=ident)
                return ins
            P.op("pe", tr, reads=[Rxb, Rc], writes=[Rpm])
            if evac_eng == "dve":
                P.op("dve", lambda e: e.tensor_copy(out=xT[:, :, tb * 128:(tb + 1) * 128],
                                                    in_=pmb.rearrange("p (k t) -> p k t", k=NKC)),
                     reads=[Rpm], writes=[RxT[tb // 4]])
            else:
                P.op("act", lambda e: e.activation(out=xT[:, :, tb * 128:(tb + 1) * 128],
                                                   in_=pmb.rearrange("p (k t) -> p k t", k=NKC), func=AF.Copy),
                     reads=[Rpm], writes=[RxT[tb // 4]])
            P.handoff(xb_res, Rsg[0] + Rsg[1])

        FILL = {"on": False}

        def make_xT_stages(tb):
            G = {}

            def cast():
                P.handoff(Rsg[0] + Rsg[1], xb_res)
                xbt, Rxb = xb.get()
                G["xb"] = (xbt, Rxb)
                P.op("act", lambda e: e.activation(out=xbt[:], in_=xres[:, tb, :], func=AF.Copy), reads=[Rxres[tb]], writes=[Rxb])

            def trans():
                xbt, Rxb = G["xb"]
                pm, Rpm = pmisc.get()
                pmb = pm[:].bitcast(BF16)
                G["pm"] = (pmb, Rpm)

                def tr(e):
                    ins = None
                    for kc in range(NKC):
                        ins = e.transpose(out=pmb[:, kc * 128:(kc + 1) * 128], in_=xbt[:, kc * 128:(kc + 1) * 128], identity=ident)
                    return ins
                P.op("pe", tr, reads=[Rxb, Rc], writes=[Rpm])

            def evac():
                pmb, Rpm = G["pm"]
                P.op("act", lambda e: e.activation(out=xT[:, :, tb * 128:(tb + 1) * 128],
                                                   in_=pmb.rearrange("p (k t) -> p k t", k=NKC), func=AF.Copy),
                     reads=[Rpm], writes=[RxT[tb // 4]])
                P.handoff(xb_res, Rsg[0] + Rsg[1])
            return cast, trans, evac

        def proj_fm(wb, Rwb, c0, t):
            pb, Rpb = (ppT if FILL["on"] else pproj).get()

            def mm(e):
                ins = None
                for kc in range(NKC):
                    ins = e.matmul(pb[:, :], lhsT=wb[:, kc, c0:c0 + 128], rhs=xT[:, kc, t * 512:(t + 1) * 512],
                                   start=(kc == 0), stop=(kc == NKC - 1))
                return ins
            P.op("pe", mm, reads=[Rwb, RxT[t]], writes=[Rpb])
            return pb, Rpb

        def proj_v(wb, Rwb, t, eng="dve"):
            pb, Rpb = proj_fm(wb, Rwb, 256, t)
            vt, Rvt = Pt.get()
            if eng == "dve":
                P.op("dve", lambda e: e.tensor_copy(out=vt[:], in_=pb[:, :]), reads=[Rpb], writes=[Rvt])
            else:
                P.op("act", lambda e: e.activation(out=vt[:], in_=pb[:, :], func=AF.Copy), reads=[Rpb], writes=[Rvt])
            pm, Rpm = (ppT if FILL["on"] else pvproj).get()
            pmb = pm[:].bitcast(BF16)

            def tr(e):
                ins = None
                for j in range(4):
                    ins = e.transpose(out=pmb[:, j * 128:(j + 1) * 128], in_=vt[:, j * 128:(j + 1) * 128], identity=ident)
                return ins
            P.op("pe", tr, reads=[Rvt, Rc], writes=[Rpm])
            pv = pmb[:, 0:512].rearrange("p (j c) -> p j c", j=4)
            if eng == "dve":
                P.op("dve", lambda e: e.tensor_copy(out=VA[:, 4 * t:4 * t + 4, 0:64], in_=pv[:, :, 0:64]),
                     reads=[Rpm, RVones], writes=[RV[0][t]])
                P.op("dve", lambda e: e.tensor_copy(out=VB[:, 4 * t:4 * t + 4, 0:64], in_=pv[:, :, 64:128]),
                     reads=[Rpm, RVones], writes=[RV[1][t]])
            else:
                P.op("act", lambda e: e.activation(out=VA[:, 4 * t:4 * t + 4, 0:64], in_=pv[:, :, 0:64], func=AF.Copy),
                     reads=[Rpm, RVones], writes=[RV[0][t]])
                P.op("act", lambda e: e.activation(out=VB[:, 4 * t:4 * t + 4, 0:64], in_=pv[:, :, 64:128], func=AF.Copy),
                     reads=[Rpm, RVones], writes=[RV[1][t]])

        def fox_prep(layer):
            P.handoff(sb_scr + ret_scr, fox_scr)
            P.op("pool", lambda e: e.memset(ones6[:], 1.0), writes=[Rones6])
            wfb = wfb2[:, layer, :, :]
            P.dma("sp", braw[:], bf_h[layer], writes=[Rbraw])
            P.op("dve", lambda e: e.tensor_scalar(out=negb[:], in0=braw[:], scalar1=-1.0, scalar2=None, op0=ALU.mult),
                 reads=[Rbraw], writes=[Rnegb])
            def kaug(e):
                e.memset(KA[64:65, :], 1.0)
                return e.memset(KB[64:65, :], 1.0)
            P.op("pool", kaug, writes=[RKaug[0], RKaug[1]] + RK[1])
            prev = None
            for t in range(NT):
                pm, Rpm = pmisc.get()

                def mm(e, pm=pm, t=t):
                    ins = None
                    for kc in range(NKC):
                        ins = e.matmul(pm[0:6, :], lhsT=wfb[:, kc, :], rhs=xT[:, kc, t * 512:(t + 1) * 512],
                                       start=(kc == 0), stop=(kc == NKC - 1))
                    return ins
                P.op("pe", mm, reads=[Rwf, RxT[t]], writes=[Rpm])
                f1, Rf1 = ft1.get()
                P.op("act", lambda e, pm=pm, f1=f1: e.activation(out=f1[:], in_=pm[0:6, :], func=AF.Exp, bias=negb[:], scale=-1.0),
                     reads=[Rpm, Rnegb], writes=[Rf1])
                P.op("act", lambda e, f1=f1: e.activation(out=f1[:], in_=f1[:], func=AF.Ln, bias=1.0, scale=1.0),
                     reads=[Rf1], writes=[Rf1])
                cs, Rcs = cs_tmp.get()
                if prev is None:
                    init = 0.0
                    rd = [Rf1, Rones6]
                else:
                    init = prev[0][:, 511:512]
                    rd = [Rf1, Rones6, prev[1]]
                P.op("dve", lambda e, cs=cs, f1=f1, init=init: e.tensor_tensor_scan(
                    out=cs[:], data0=ones6[:], data1=f1[:], initial=init, op0=ALU.mult, op1=ALU.add),
                    reads=rd, writes=[Rcs])
                prev = (cs, Rcs)
                P.op("dve", lambda e, cs=cs, t=t: e.tensor_scalar(out=negcspT[:, t * 512:(t + 1) * 512], in0=cs[:], scalar1=-1.0,
                                                                  scalar2=None, op0=ALU.mult),
                     reads=[Rcs], writes=[Rncsp[t]])
                pm2, Rpm2 = pmisc.get()

                def tr(e, pm2=pm2, cs=cs):
                    ins = None
                    for j in range(4):
                        ins = e.transpose(out=pm2[:, j * 6:(j + 1) * 6], in_=cs[:, j * 128:(j + 1) * 128], identity=identf[0:6, 0:6])
                    return ins
                P.op("pe", tr, reads=[Rcs, Rc], writes=[Rpm2])
                P.op("dve", lambda e, pm2=pm2, t=t: e.tensor_copy(out=csp_tok[:, 4 * t:4 * t + 4, :],
                                                                  in_=pm2[:, 0:24].rearrange("p (j c) -> p j c", j=4)),
                     reads=[Rpm2], writes=[Rcsptok[t]])

        def inproj_headwise(wb, Rwb, kind):
            def setup():
                if kind == "sb":
                    P.handoff(ret_scr + fox_scr, sb_scr)
                    P.dma("pool", KA[64:80, :], msel_h, writes=[RKaug[0]] + RK[1])
                    P.dma("pool", KB[64:80, :], msel_h, writes=[RKaug[1]])

            def q_item(t):
                ts = slice(t * 512, (t + 1) * 512)
                pb, Rpb = proj_fm(wb, Rwb, 0, t)
                P.op("act", lambda e: e.activation(out=QA[0:64, ts], in_=pb[0:64, :], func=AF.Copy, scale=0.125),
                     reads=[Rpb], writes=[RQ[0][t]])
                P.op("dve", lambda e: e.tensor_scalar(out=QB[0:64, ts], in0=pb[64:128, :], scalar1=0.125, scalar2=None, op0=ALU.mult),
                     reads=[Rpb], writes=[RQ[1][t]])

            def k_item(t):
                ts = slice(t * 512, (t + 1) * 512)
                pb, Rpb = proj_fm(wb, Rwb, 128, t)
                P.op("act", lambda e: e.activation(out=KA[0:64, ts], in_=pb[0:64, :], func=AF.Copy), reads=[Rpb], writes=[RK[0][t]])
                P.op("dve", lambda e: e.tensor_copy(out=KB[0:64, ts], in_=pb[64:128, :]), reads=[Rpb], writes=[RK[1][t]])

            def g_item(t):
                ts = slice(t * 512, (t + 1) * 512)
                pb, Rpb = proj_fm(wb, Rwb, 384, t)
                P.op("act", lambda e: e.activation(out=sgA[0:64, ts], in_=pb[0:64, :], func=AF.Silu), reads=[Rpb], writes=[Rsg[0][t]])
                P.op("act", lambda e: e.activation(out=sgB[0:64, ts], in_=pb[64:128, :], func=AF.Silu), reads=[Rpb], writes=[Rsg[1][t]])

            qkv = [[(lambda t=t: q_item(t)), (lambda t=t: k_item(t)), (lambda t=t: proj_v(wb, Rwb, t))] for t in range(NT)]
            gate = [(lambda t=t: g_item(t)) for t in range(NT)]
            return setup, qkv, gate

        def inproj_ret(wb, Rwb, rp):
            def setup():
                P.handoff(sb_scr + fox_scr, ret_scr)
                if rp == 0:
                    P.dma("sp", iota[:], iota_h, writes=[Riota])

            def qk_item(t, which):
                ts = slice(t * 512, (t + 1) * 512)
                c0, dst, Rdst, Raug = ((0, QA, RQ, RQaug[0]), (128, KA, RK, RKaug[0]))[which]
                if which == 0:
                    P.dma("sp", rot[0][:], cos_h[:, ts], writes=[Rrot])
                    P.dma("sp", rot[1][:], sin_h[:, ts], writes=[Rrot])
                pb, Rpb = proj_fm(wb, Rwb, c0, t)
                dt_, Rd = f32t.get()
                col = rp * 10 + which
                bcol = rp * 10 + 2 + which * 4 + t
                P.op("act", lambda e: e.activation(out=dt_[:], in_=iota[:], func=AF.Exp, bias=dec[:, bcol:bcol + 1],
                                                   scale=dec[:, col:col + 1]),
                     reads=[Rc, Riota], writes=[Rd])
                qd, Rqd = f32t.get()
                P.op("dve", lambda e: e.tensor_tensor(out=qd[:], in0=pb[:, :], in1=dt_[:], op=ALU.mult), reads=[Rpb, Rd], writes=[Rqd])
                u, Ru = f32t.get()

                def swapmul(e):
                    ins = None
                    for base in (0, 64):
                        e.tensor_tensor(out=u[base:base + 32, :], in0=qd[base + 32:base + 64, :],
                                        in1=rot[1][base + 32:base + 64, :], op=ALU.mult)
                        ins = e.tensor_tensor(out=u[base + 32:base + 64, :], in0=qd[base:base + 32, :],
                                              in1=rot[1][base:base + 32, :], op=ALU.mult)
                    return ins
                P.op("dve", swapmul, reads=[Rqd, Rrot], writes=[Ru])
                P.op("dve", lambda e: e.tensor_tensor(out=qd[:], in0=qd[:], in1=rot[0][:], op=ALU.mult),
                     reads=[Rqd, Rrot, Ru], writes=[Rqd])
                P.op("dve", lambda e: e.tensor_tensor(out=dst[:, ts], in0=qd[:], in1=u[:], op=ALU.add),
                     reads=[Rqd, Ru], writes=[Rdst[0][t], Rdst[1][t], Raug])

            def g_item(t):
                ts = slice(t * 512, (t + 1) * 512)
                pb, Rpb = proj_fm(wb, Rwb, 384, t)
                P.op("act", lambda e: e.activation(out=sgA[:, ts], in_=pb[:, :], func=AF.Silu), reads=[Rpb], writes=[Rsg[0][t], Rsg[1][t]])

            qkv = [[(lambda t=t: qk_item(t, 0)), (lambda t=t: qk_item(t, 1)), (lambda t=t: proj_v(wb, Rwb, t, "act"))] for t in range(NT)]
            gate = [(lambda t=t: g_item(t)) for t in range(NT)]
            return setup, qkv, gate

        def flatten(items, depth):
            out = []
            n = len(items)
            for i in range(n + depth):
                if i < n:
                    out.append(items[i][0])
                if i >= depth:
                    out.append(items[i - depth][1])
            return out

        def run_merged(main, fill):
            n, m = len(main), len(fill)
            if m == 0:
                for f in main:
                    f()
                return
            stride = max(1, n // (m + 1))
            k = 0
            for i, f in enumerate(main):
                f()
                if k < m and (i + 1) % stride == 0:
                    fill[k]()
                    k += 1
            while k < m:
                fill[k]()
                k += 1

        def pipeline(items, depth):
            n = len(items)
            for i in range(n + depth):
                if i < n:
                    items[i][0]()
                if i >= depth:
                    items[i - depth][1]()

        def fox_items(hi, gh, hp, t):
            Q, K, V, sg = Qh[hi], Kh[hi], Vh[hi], sgh[hi]
            r0 = hi * 64
            items = []
            if True:
                nkb = 4 * (t + 1)
                tstate = {}
                for kb in range(nkb):
                    j = kb - 4 * t
                    c0 = max(j, 0) * 128
                    ks = slice(kb * 128, (kb + 1) * 128)
                    st_ = {}

                    def A(st_=st_, ks=ks, c0=c0, j=j, t=t, kb=kb):
                        ps, Rps = pbig.get()
                        st_["ps"] = (ps, Rps)

                        def mm(e):
                            ins = e.matmul(ps[:, c0:512], lhsT=K[0:65, ks], rhs=Q[0:65, t * 512 + c0:(t + 1) * 512],
                                           start=True, stop=(j < 0))
                            if j >= 0:
                                ins = e.matmul(ps[:, c0:c0 + 128], lhsT=ident, rhs=negfox, start=False, stop=True,
                                               skip_group_check=True)
                            return ins
                        P.op("pe", mm, reads=[RK[hi][kb // 4], RKaug[hi], RQ[hi][t], RQaug[hi], Rc], writes=[Rps])

                    def B(st_=st_, tstate=tstate, c0=c0, kb=kb, nkb=nkb, t=t):
                        ps, Rps = st_["ps"]
                        if kb == 0:
                            tstate["po"] = pobank.get()
                        po, Rpo = tstate["po"]
                        pt, Rpt = Pt.get()
                        P.op("act", lambda e: e.activation(out=pt[:, c0:512], in_=ps[:, c0:512], func=AF.Exp,
                                                           bias=csp_tok[:, kb, gh:gh + 1], scale=1.0),
                             reads=[Rps, Rcsptok[kb // 4]], writes=[Rpt])
                        P.op("pe", lambda e: e.matmul(po[:, c0:512], lhsT=V[:, kb, :], rhs=pt[:, c0:512], start=(kb == 0),
                                                      stop=(kb == nkb - 1), skip_group_check=True),
                             reads=[Rpt, RV[hi][kb // 4], RVones], writes=[Rpo])
                        if kb == nkb - 1:
                            ts = slice(t * 512, (t + 1) * 512)
                            rc, Rrc = f32t.get()
                            P.op("dve", lambda e: e.reciprocal(out=rc[0:64, :], in_=po[64:128, :]), reads=[Rpo], writes=[Rrc])
                            P.op("dve", lambda e: e.tensor_tensor(out=rc[0:64, :], in0=rc[0:64, :], in1=sg[0:64, ts], op=ALU.mult),
                                 reads=[Rrc, Rsg[hi][t]], writes=[Rrc])
                            P.op("dve", lambda e: e.tensor_tensor(out=yT[r0:r0 + 64, hp, ts], in0=po[0:64, :], in1=rc[0:64, :],
                                                                  op=ALU.mult),
                                 reads=[Rpo, Rrc], writes=[RyT[hp][t]])
                    items.append((A, B))
            return items

        def sb_stream(hi, hp, t, raw=False):
            Q, K, V, sg = Qh[hi], Kh[hi], Vh[hi], sgh[hi]
            r0 = hi * 64
            stream = []
            pcs_box = {}
            if True:
                nkb = 4 * (t + 1)
                items = []
                for kb in range(nkb):
                    j = kb - 4 * t
                    c0 = max(j, 0) * 128
                    r = nkb - 1 - kb
                    ks = slice(kb * 128, (kb + 1) * 128)
                    st_ = {}

                    def A(st_=st_, ks=ks, c0=c0, t=t, kb=kb):
                        pz, Rpz = sbbig.get()
                        st_["pz"] = (pz, Rpz)
                        P.op("pe", lambda e: e.matmul(pz[:, c0:512], lhsT=K[0:64, ks], rhs=Q[0:64, t * 512 + c0:(t + 1) * 512],
                                                      start=True, stop=True),
                             reads=[RK[hi][kb // 4], RQ[hi][t]], writes=[Rpz])

                    def B(st_=st_, c0=c0, kb=kb, j=j, r=r, nkb=nkb):
                        pz, Rpz = st_["pz"]
                        if kb == 0:
                            pcs_box["p"] = pobank.get()
                        pcs, Rpcs = pcs_box["p"]
                        eb, Reb = Ebuf.get()
                        P.op("act", lambda e: e.activation(out=eb[:, c0:512], in_=pz[:, c0:512], func=AF.Exp),
                             reads=[Rpz], writes=[Reb])
                        P.op("act", lambda e: e.activation(out=SPb[kb][:, c0:512], in_=eb[:, c0:512], func=AF.Ln, bias=1.0, scale=1.0),
                             reads=[Reb], writes=[RSP[kb]])
                        if j >= 0:
                            P.op("dve", lambda e: e.tensor_tensor(out=SPb[kb][:, c0:c0 + 128], in0=SPb[kb][:, c0:c0 + 128],
                                                                   in1=m01s, op=ALU.mult),
                                 reads=[RSP[kb], Rc], writes=[RSP[kb]])
                        P.op("pe", lambda e: e.matmul(pcs[0:16, c0:512], lhsT=oh[:, 15 - kb:31 - kb], rhs=SPb[kb][:, c0:512],
                                                      start=(kb == 0), stop=(kb == nkb - 1), skip_group_check=True),
                             reads=[RSP[kb], Rc], writes=[Rpcs])
                    items.append((A, B))
                p1_items = items
                stream += flatten(items, 2)

                def cscopy():
                    pcs, Rpcs = pcs_box["p"]
                    P.op("dve", lambda e: e.tensor_copy(out=Q[64:80, t * 512:(t + 1) * 512], in_=pcs[0:16, :]),
                         reads=[Rpcs], writes=[RQaug[hi]])
                stream.append(cscopy)
                tstate = {}
                items = []
                for kb in range(nkb):
                    j = kb - 4 * t
                    c0 = max(j, 0) * 128
                    r = nkb - 1 - kb
                    ks = slice(kb * 128, (kb + 1) * 128)
                    st_ = {}

                    def A2(st_=st_, ks=ks, c0=c0, j=j, r=r, t=t, kb=kb):
                        pa, Rpa = sbbig.get()
                        st_["pa"] = (pa, Rpa)

                        def mm(e):
                            e.matmul(pa[:, c0:512], lhsT=K[0:80, ks], rhs=Q[0:80, t * 512 + c0:(t + 1) * 512], start=True, stop=False)
                            ins = e.matmul(pa[:, c0:512], lhsT=neguinc, rhs=SPb[kb][:, c0:512], start=False, stop=(j < 0),
                                           skip_group_check=True)
                            if j >= 0:
                                ins = e.matmul(pa[:, c0:c0 + 128], lhsT=ident, rhs=negsb, start=False, stop=True, skip_group_check=True)
                            return ins
                        P.op("pe", mm, reads=[RK[hi][kb // 4], RKaug[hi], RQ[hi][t], RQaug[hi], RSP[kb], Rc], writes=[Rpa])

                    def B2(st_=st_, tstate=tstate, c0=c0, kb=kb, nkb=nkb, t=t):
                        pa, Rpa = st_["pa"]
                        if kb == 0:
                            tstate["po"] = pobank.get()
                        po, Rpo = tstate["po"]
                        pt, Rpt = Pt.get()
                        P.op("act", lambda e: e.activation(out=pt[:, c0:512], in_=pa[:, c0:512], func=AF.Exp),
                             reads=[Rpa], writes=[Rpt])
                        P.op("pe", lambda e: e.matmul(po[:, c0:512], lhsT=V[:, kb, :], rhs=pt[:, c0:512], start=(kb == 0),
                                                      stop=(kb == nkb - 1), skip_group_check=True),
                             reads=[Rpt, RV[hi][kb // 4], RVones], writes=[Rpo])
                        if kb == nkb - 1:
                            ts = slice(t * 512, (t + 1) * 512)
                            P.op("dve", lambda e: e.tensor_tensor(out=yT[r0:r0 + 64, hp, ts], in0=po[0:64, :], in1=sg[0:64, ts],
                                                                  op=ALU.mult),
                                 reads=[Rpo, Rsg[hi][t]], writes=[RyT[hp][t]])
                    items.append((A2, B2))
                stream += flatten(items, 2)
            if raw:
                return p1_items, cscopy, items
            return stream

        def sb_pair(hp):
            units = [sb_stream(hi, hp, t, raw=True) for t in range(NT) for hi in range(2)]
            pipeline(units[0][0], 2)
            units[0][1]()
            for i in range(len(units)):
                p2 = units[i][2]
                p1n = units[i + 1][0] if i + 1 < len(units) else []
                n = max(len(p2), len(p1n))
                for k in range(n + 1):
                    if k < len(p2):
                        p2[k][0]()
                    if k < len(p1n):
                        p1n[k][0]()
                    if SB_DUMMY:
                        jb, Rjb = pproj.items[0]

                        def dmm(e, jb=jb):
                            ins = None
                            for _ in range(SB_DUMMY):
                                ins = e.matmul(jb[:, :], lhsT=ident, rhs=c128[:, 0:512], start=True, stop=True)
                            return ins
                        P.op("pe", dmm, reads=[Rc], writes=[Rjb])
                    if 0 <= k - 1 < len(p2):
                        p2[k - 1][1]()
                    if 0 <= k - 1 < len(p1n):
                        p1n[k - 1][1]()
                if i + 1 < len(units):
                    units[i + 1][1]()

        def ret_stream(rp, hp, tg, gstate, raw=False):
            def ktok():
                pm, Rpm = ppT.get()
                pmb = pm[:].bitcast(BF16)

                def tr(e):
                    ins = None
                    for jj in range(4):
                        c = 4 * tg + jj
                        ins = e.transpose(out=pmb[:, jj * 128:(jj + 1) * 128], in_=KA[:, c * 128:(c + 1) * 128], identity=ident)
                    return ins
                P.op("pe", tr, reads=[RK[0][tg], RK[1][tg], Rc], writes=[Rpm])
                P.op("act", lambda e: e.activation(out=Ktok[:, 4 * tg:4 * tg + 4, :],
                                                   in_=pmb[:, 0:512].rearrange("p (j c) -> p j c", j=4), func=AF.Copy),
                     reads=[Rpm], writes=[RKtok[tg]])
            items = [ret_chunk_item(rp, hp, c, gstate) for c in range(4 * tg, 4 * tg + 4)]
            if raw:
                return ktok, items
            return [ktok] + flatten(items, 1)

        def ret_chunk_item(rp, hp, c, gstate):
            cg, ci = c // 4, c % 4
            cs_ = slice(c * 128, (c + 1) * 128)
            st_ = {}

            def A():
                pst, Rpst = pbig.get()
                st_["p"] = (pst, Rpst, pst, Rpst)

                pS, RpS = pmisc.items[0]

                def mm(e):
                    e.matmul(pst[:, 0:128], lhsT=KA[0:64, cs_], rhs=QA[0:64, cs_], start=True, stop=True)
                    if c == 0:
                        e.matmul(pS[:, 0:128], lhsT=zeros128, rhs=c128[:, 0:128], start=True, stop=False, skip_group_check=True)
                    e.matmul(pS[:, 0:64], lhsT=Ktok[:, c, :], rhs=VA[:, c, 0:64], start=False, stop=(c == NB - 1),
                             skip_group_check=True)
                    e.matmul(pS[:, 64:128], lhsT=Ktok[:, c, :], rhs=VB[:, c, 0:64], start=False, stop=(c == NB - 1),
                             skip_group_check=True)
                    return e.matmul(pst[:, 128:256], lhsT=KA[64:128, cs_], rhs=QA[64:128, cs_], start=True, stop=True)
                P.op("pe", mm, reads=[RK[0][cg], RK[1][cg], RQ[0][cg], RQ[1][cg], RKtok[cg], RV[0][cg], RV[1][cg], Rc],
                     writes=[Rpst, RpS])
                if c < NB - 1:
                    P.op("act", lambda e: e.activation(out=Sall[:, c + 1, :], in_=pS[:, 0:128], func=AF.Copy),
                         reads=[RpS], writes=[RSall[c + 1]])

            def B():
                pst, Rpst, pst2, Rpst2 = st_["p"]
                if ci == 0:
                    gstate["po"] = pobank.get()
                po, Rpo = gstate["po"]
                pt, Rpt = Pt.get()

                P.op("dve", lambda e: e.tensor_tensor(out=pt[:, 0:256], in0=pst[:, 0:256], in1=mask2[:], op=ALU.mult),
                     reads=[Rpst, Rc], writes=[Rpt])

                def om(e):
                    o0 = ci * 128
                    e.matmul(po[:, o0:o0 + 64], lhsT=pt[:, 0:128], rhs=VA[:, c, 0:64], start=True, stop=(c == 0))
                    if c > 0:
                        e.matmul(po[:, o0:o0 + 64], lhsT=QA[0:64, cs_], rhs=Sall[0:64, c, 0:64], start=False, stop=True)
                    ins = e.matmul(po[:, o0 + 64:o0 + 128], lhsT=pt[:, 128:256], rhs=VB[:, c, 0:64], start=True, stop=(c == 0))
                    if c > 0:
                        ins = e.matmul(po[:, o0 + 64:o0 + 128], lhsT=QA[64:128, cs_], rhs=Sall[64:128, c, 64:128], start=False, stop=True)
                    return ins
                P.op("pe", om, reads=[Rpt, RV[0][cg], RV[1][cg], RQ[0][cg], RQ[1][cg], RSall[c]], writes=[Rpo])
                pend = gstate.setdefault("pend", [])
                for ent in list(pend):
                    ent[0] -= 1
                    if ent[0] <= 0:
                        pend.remove(ent)
                        ent[1]()
                if ci == 3:
                    st1, st2, st3 = ret_gn(rp, hp, cg, po, Rpo)
                    st1()
                    pend.append([1, st2])
                    pend.append([2, st3])
            return (A, B)

        def ret_gn(rp, hp, cg, po, Rpo):
            G = {}

            def stage1():
                if True:
                    s8, Rs8 = st8.get()
                    pov = po[:].rearrange("p (g d) -> p g d", g=8)
                    def gstats(e, s8=s8, pov=pov):
                        ins = None
                        for g in range(8):
                            ins = e.bn_stats(out=s8[:, g, :], in_=pov[:, g, :])
                        return ins
                    P.op("dve", gstats, reads=[Rpo], writes=[Rs8])
                    m8, Rm8 = mv8.get()
                    def aggr(e, s8=s8, m8=m8):
                        ins = None
                        for g in range(8):
                            ins = e.bn_aggr(out=m8[:, g, :], in_=s8[:, g, :])
                        return ins
                    P.op("dve", aggr, reads=[Rs8], writes=[Rm8])
                    r8, Rr8 = rs8.get()
                    P.op("act", lambda e, r8=r8, m8=m8: e.activation(out=r8[:], in_=m8[:, :, 1], func=AF.Ln, bias=GN_EPS, scale=1.0),
                         reads=[Rm8], writes=[Rr8])
                    P.op("act", lambda e, r8=r8: e.activation(out=r8[:], in_=r8[:], func=AF.Exp, scale=-0.5),
                         reads=[Rr8], writes=[Rr8])
                    G.update(pov=pov, m8=m8, Rm8=Rm8, r8=r8, Rr8=Rr8)

            def stage2():
                if True:
                    pov, m8, Rm8, r8, Rr8 = G["pov"], G["m8"], G["Rm8"], G["r8"], G["Rr8"]
                    t1, Rt1 = f32t.get()
                    t1v = t1[:].rearrange("p (g d) -> p g d", g=8)

                    def gnorm(e, t1v=t1v, pov=pov, m8=m8, r8=r8):
                        ins = None
                        for g in range(8):
                            ins = e.tensor_scalar(out=t1v[:, g, :], in0=pov[:, g, :], scalar1=m8[:, g, 0:1], scalar2=r8[:, g:g + 1],
                                                  op0=ALU.subtract, op1=ALU.mult)
                        return ins
                    P.op("dve", gnorm, reads=[Rpo, Rm8, Rr8], writes=[Rt1])
                    pt2, Rpt2 = Pt.get()

                    def ggain(e, pt2=pt2, t1=t1):
                        ins = None
                        for cc in range(4):
                            ins = e.tensor_tensor(out=pt2[:, cc * 128:(cc + 1) * 128], in0=t1[:, cc * 128:(cc + 1) * 128],
                                                  in1=gng[:, rp * 128:(rp + 1) * 128], op=ALU.mult)
                        return ins
                    P.op("dve", ggain, reads=[Rt1, Rgng], writes=[Rpt2])
                    pm, Rpm = ppT.get()
                    pmb = pm[:].bitcast(BF16)

                    def tr(e, pmb=pmb, pt2=pt2):
                        ins = None
                        for jj in range(4):
                            ins = e.transpose(out=pmb[:, jj * 128:(jj + 1) * 128], in_=pt2[:, jj * 128:(jj + 1) * 128], identity=ident)
                        return ins
                    P.op("pe", tr, reads=[Rpt2, Rc], writes=[Rpm])
                    G.update(pmb=pmb, Rpm=Rpm)

            def stage3():
                if True:
                    pmb, Rpm = G["pmb"], G["Rpm"]
                    ts = slice(cg * 512, (cg + 1) * 512)
                    P.op("dve", lambda e, pmb=pmb, ts=ts: e.tensor_tensor(out=yT[:, hp, ts], in0=pmb[:, 0:512], in1=sgA[:, ts],
                                                                          op=ALU.mult),
                         reads=[Rpm, Rsg[0][cg], Rsg[1][cg]], writes=[RyT[hp][cg]])
            return [stage1, stage2, stage3]

        def ln_item(s, tb, last, w0, Rw0, w1, Rw1):
            tks = slice(tb * 128, (tb + 1) * 128)
            st_ = {}

            def S0():
                pbs = []
                for half, (wb, Rwb) in enumerate(((w0, Rw0), (w1, Rw1))):
                    pb, Rpb = pln.get()
                    pbs.append((pb, Rpb))

                    def mm(e, pb=pb, wb=wb):
                        ins = None
                        for ec in range(NKC):
                            ins = e.matmul(pb[:, :], lhsT=yT[:, ec, tks], rhs=wb[:, ec, :], start=(ec == 0), stop=(ec == NKC - 1))
                        return ins
                    P.op("pe", mm, reads=[Rwb] + [RyT[c][tb // 4] for c in range(8)], writes=[Rpb])
                st_["pbs"] = pbs

            def S1():
                z, Rz = zt.get()
                for half, (pb, Rpb) in enumerate(st_["pbs"]):
                    hs = slice(half * 512, (half + 1) * 512)
                    P.op("dve", lambda e, pb=pb, hs=hs: e.scalar_tensor_tensor(
                        out=z[:, hs], in0=xres[:, tb, hs], scalar=ALPHA, in1=pb[:, :], op0=ALU.mult, op1=ALU.add),
                        reads=[Rpb, Rxres[tb]], writes=[Rz])
                ls, Rls = lnst.get()

                def lnstats(e):
                    e.bn_stats(out=ls[:, 0, :], in_=z[:, 0:512])
                    return e.bn_stats(out=ls[:, 1, :], in_=z[:, 512:1024])
                P.op("dve", lnstats, reads=[Rz], writes=[Rls])
                lm, Rlm = lnmv.get()
                P.op("dve", lambda e: e.bn_aggr(out=lm[:, 0:2], in_=ls[:].rearrange("p g s -> p (g s)")), reads=[Rls], writes=[Rlm])
                st_["z"] = (z, Rz, lm, Rlm)

            def S2():
                z, Rz, lm, Rlm = st_["z"]
                P.op("act", lambda e: e.activation(out=lm[:, 2:3], in_=lm[:, 1:2], func=AF.Ln, bias=LN_EPS, scale=1.0),
                     reads=[Rlm], writes=[Rlm])
                P.op("act", lambda e: e.activation(out=lm[:, 2:3], in_=lm[:, 2:3], func=AF.Exp, scale=-0.5),
                     reads=[Rlm], writes=[Rlm])

            def S3():
                z, Rz, lm, Rlm = st_["z"]
                P.op("dve", lambda e: e.scalar_tensor_tensor(out=z[:], in0=z[:], scalar=lm[:, 0:1], in1=lngain[:],
                                                             op0=ALU.subtract, op1=ALU.mult),
                     reads=[Rz, Rlm, Rlng], writes=[Rz])
                P.op("dve", lambda e: e.scalar_tensor_tensor(out=xres[:, tb, :], in0=z[:], scalar=lm[:, 2:3], in1=lnbias[:],
                                                             op0=ALU.mult, op1=ALU.add),
                     reads=[Rz, Rlm, Rlnb], writes=[Rxres[tb]])

            if last:
                def S4():
                    P.dma("sp", out_h[s, tks, :], xres[:, tb, :], reads=[Rxres[tb]])
                nop = lambda: None
                return [S0, S1, S2, S3, S4, nop, nop]
            cast, trans, evac = make_xT_stages(tb)
            return [S0, S1, S2, S3, cast, trans, evac]

        def outproj_ln(layer, s, last):
            w0, Rw0 = load_w(wo_h[layer, 0])
            w1, Rw1 = load_w(wo_h[layer, 1])
            P.handoff(qk_res, ln_res)
            P.dma("sp", lngain[:], lg_h[layer].broadcast_to([128, D]), writes=[Rlng])
            P.dma("sp", lnbias[:], lb_h[layer].broadcast_to([128, D]), writes=[Rlnb])
            items = []
            for tb in range(NB):
                items.append(ln_item(s, tb, last, w0, Rw0, w1, Rw1))
            n = len(items)
            order = [0, 1, 2, 3, 4, 6, 5]
            for i in range(n + 6):
                for sidx in order:
                    if 0 <= i - sidx < n:
                        items[i - sidx][sidx]()
            P.handoff(ln_res, qk_res)
            P.op("pool", ones_init, writes=[RVones])

        def main_program(stage):
            for s in range(n_seq):
                for tb in range(NB):
                    P.dma("sp", xres[:, tb, :], x_h[s, tb * 128:(tb + 1) * 128, :], writes=[Rxres[tb]])
                for tb in range(NB):
                    make_xT(tb)
                for layer in range(n_layers):
                    P.dma("sp", gng[:], gn_h[layer].broadcast_to([128, 384]), writes=[Rgng])
                    nxt = load_w(wp_h[layer, 0])
                    fox_prep(layer)
                    stage("fox_prep")
                    prefilled = set()

                    def build_inproj(hp_, wb_, Rwb_):
                        if hp_ < 3:
                            return inproj_headwise(wb_, Rwb_, "fox")
                        elif hp_ < 6:
                            return inproj_ret(wb_, Rwb_, hp_ - 3)
                        return inproj_headwise(wb_, Rwb_, "sb")

                    for hp in range(8):
                        wb, Rwb = nxt
                        if hp < 7:
                            nxt = load_w(wp_h[layer, hp + 1])
                        setup, qkv, gate = build_inproj(hp, wb, Rwb)
                        if hp in prefilled:
                            pass
                        elif INTERLEAVE:
                            setup()
                            for f in qkv[0] + gate:
                                f()
                        else:
                            setup()
                            for t in range(NT):
                                for f in qkv[t]:
                                    f()
                            for f in gate:
                                f()
                        if hp < 3:
                            for hi in range(2):
                                gh = 2 * hp + hi
                                P.dma("sp", Qh[hi][64:65, :], negcspT[gh:gh + 1, :], reads=Rncsp, writes=[RQaug[hi]] + RQ[1])
                        gstate = {}
                        if not INTERLEAVE and hp < 3:
                            items = []
                            for t in range(NT):
                                items += fox_items(0, 2 * hp, hp, t) + fox_items(1, 2 * hp + 1, hp, t)
                            pipeline(items, 2)
                        elif not INTERLEAVE and hp < 6:
                            pre, items = [], []
                            for t in range(NT):
                                st = ret_stream(hp - 3, hp, t, gstate, raw=True)
                                pre.append(st[0])
                                items += st[1]
                            for f in pre:
                                f()
                            if RET_FILL and hp + 1 <= 6:
                                nsetup, nqkv, ngate = build_inproj(hp + 1, nxt[0], nxt[1])
                                prefilled.add(hp + 1)
                                if hp + 1 < 6:
                                    nsetup()
                                main = flatten(items, RET_DEPTH)
                                bpos = {}
                                for c in range(NB):
                                    bpos[c] = main.index(items[c][1])
                                fills = {}
                                for t in range(NT - 1):
                                    base = bpos[4 * t + 3]
                                    for k_, f in enumerate(nqkv[t]):
                                        fills.setdefault(min(base + 2 + 3 * k_, len(main) - 1), []).append(f)
                                FILLON = FILL
                                for idx_, f in enumerate(main):
                                    f()
                                    for g_ in fills.get(idx_, []):
                                        FILLON["on"] = True
                                        g_()
                                        FILLON["on"] = False
                                for ent in sorted(gstate.get("pend", []), key=lambda x: x[0]):
                                    ent[1]()
                                if hp + 1 == 6:
                                    nsetup()
                                for f in nqkv[NT - 1] + ngate:
                                    f()
                            else:
                                pipeline(items, RET_DEPTH)
                                for ent in sorted(gstate.get("pend", []), key=lambda x: x[0]):
                                    ent[1]()
                        if not INTERLEAVE and hp >= 6:
                            sb_pair(hp)
                        for t in range(NT):
                            if not INTERLEAVE:
                                break
                            if hp < 3:
                                main = flatten(fox_items(0, 2 * hp, hp, t) + fox_items(1, 2 * hp + 1, hp, t), 2)
                            elif hp < 6:
                                main = ret_stream(hp - 3, hp, t, gstate)
                            else:
                                main = sb_stream(0, hp, t) + sb_stream(1, hp, t)
                            run_merged(main, qkv[t + 1] if (INTERLEAVE and t + 1 < NT) else [])
                        stage("pair %d" % hp)
                    if dbg_y and s == 0 and layer == 0:
                        P.dma("sp", dbgy_h, yT[:].rearrange("p k t -> p (k t)"), reads=[RyT[c][t] for c in range(8) for t in range(NT)])
                    outproj_ln(layer, s, last=(layer == n_layers - 1))

        class StopBuild(Exception):
            pass
        stg = {"n": 0}

        def stage(name):
            stg["n"] += 1
            if dbg_stop is not None and stg["n"] == dbg_stop:
                print("debug stop at stage", stg["n"], name)
                raise StopBuild()

        try:
            main_program(stage)
        except StopBuild:
            for tb in range(NB):
                P.dma("sp", out_h[0, tb * 128:(tb + 1) * 128, :], xres[:, tb, :], reads=[Rxres[tb]])
        P.final_wait_all("sp")
        P.emit()
    return nc


def host_consts():
    k = np.arange(128)[:, None]
    q = np.arange(128)[None, :]
    ident = (k == q).astype(np.float32)
    m01 = (k <= q).astype(np.float32)
    m01s = (k < q).astype(np.float32)
    negfox = np.where(k > q, NEG, 0.0).astype(np.float32)
    negsb = np.where(k >= q, NEG, 0.0).astype(np.float32)
    neguinc = np.where(k >= q, -1.0, 0.0).astype(np.float32)
    negones = -np.ones((128, 128), np.float32)
    c128 = np.concatenate([ident, m01, m01s, negfox, negsb, neguinc, negones, np.zeros((128, 128), np.float32)], axis=1)
    oh = np.zeros((128, 32), np.float32)
    oh[:, 15] = 1.0
    msel = np.where(np.arange(16)[:, None] > (np.arange(S)[None, :] // 128), -1.0, 0.0).astype(np.float32)
    half = 32
    inv_freq = (1.0 / (10000.0 ** (np.arange(half, dtype=np.float32) / half))).astype(np.float32)
    pos = np.arange(S, dtype=np.float32)
    ang = pos[None, :] * inv_freq[:, None]
    cos32 = np.cos(ang).astype(np.float32)
    sin32 = np.sin(ang).astype(np.float32)
    cost = np.tile(cos32, (4, 1))
    sint = np.concatenate([sin32, -sin32, sin32, -sin32], axis=0)
    iota = np.tile(np.arange(512, dtype=np.float32)[None, :], (128, 1))
    log_g = np.log(1.0 - 2.0 ** (-5.0 - np.arange(6, dtype=np.float64)))
    dec = np.zeros((128, 30), np.float32)
    for rp in range(3):
        for p in range(128):
            lg = log_g[2 * rp + (1 if p >= 64 else 0)]
            dec[p, rp * 10 + 0] = lg
            dec[p, rp * 10 + 1] = -lg
            for t in range(4):
                dec[p, rp * 10 + 2 + t] = lg * 512 * t
                dec[p, rp * 10 + 6 + t] = -lg * 512 * t + math.log(0.125)
    return dict(c128=c128, oh=oh, msel=msel, cost=cost, sint=sint, iota=iota, dec=dec)


def host_weights(w_in, w_out, b_fgate, ret_gn_gain, ln_gain, ln_bias):
    L = w_in.shape[0]
    w4 = np.ascontiguousarray(w_in[:, :, :4096]).reshape(L, NKC, 128, 4, 8, 128)
    wp = np.ascontiguousarray(w4.transpose(0, 4, 2, 1, 3, 5)).reshape(L, 8, 128, NKC * 512)
    wf = np.ascontiguousarray(w_in[:, :, 4096:4102].reshape(L, NKC, 128, 6).transpose(0, 2, 1, 3)).reshape(L, 128, NKC * 6)
    wo4 = w_out.reshape(L, NKC, 128, 2, 512)
    wo = np.ascontiguousarray(wo4.transpose(0, 3, 2, 1, 4)).reshape(L, 2, 128, NKC * 512)
    return dict(wp=wp, wf=wf, wo=wo,
                bfg=np.ascontiguousarray(b_fgate.reshape(L, 6, 1)),
                gng=np.ascontiguousarray(ret_gn_gain.reshape(L, 1, 384)),
                lng=np.ascontiguousarray(ln_gain.reshape(L, 1, D)),
                lnb=np.ascontiguousarray(ln_bias.reshape(L, 1, D)))


_NC_CACHE = {}


def kernel(x, w_in, b_fgate, ret_gn_gain, w_out, ln_gain, ln_bias):
    x = np.asarray(x, np.float32)
    n_cores = 8
    n_seq = x.shape[0] // n_cores
    shared = host_consts()
    shared.update(host_weights(np.asarray(w_in, np.float32), np.asarray(w_out, np.float32), np.asarray(b_fgate, np.float32),
                               np.asarray(ret_gn_gain, np.float32), np.asarray(ln_gain, np.float32), np.asarray(ln_bias, np.float32)))
    if "nc" not in _NC_CACHE:
        _NC_CACHE["nc"] = build(n_seq, DEPTH)
    nc = _NC_CACHE["nc"]
    in_maps = []
    for c in range(n_cores):
        m = dict(shared)
        m["x"] = np.ascontiguousarray(x[c * n_seq:(c + 1) * n_seq])
        in_maps.append(m)
    res = run_bass_kernel_spmd(nc, in_maps, core_ids=list(range(n_cores)))
    return np.concatenate([r["out"] for r in res.results], axis=0).astype(np.float32)
```

```python
import contextlib
import math
import numpy as np
import concourse.bass as bass
import concourse.mybir as mybir
from concourse.bass_utils import run_bass_kernel_spmd

F32 = mybir.dt.float32
BF16 = mybir.dt.bfloat16
AF = mybir.ActivationFunctionType
ALU = mybir.AluOpType

S = 2048
D = 1024
NB = 16
NT = 4
NKC = 8
DEPTH = 2
ALPHA = (2 * DEPTH) ** 0.25
LN_EPS = 1e-5
GN_EPS = 1e-5
NEG = -30000.0
RET_DEPTH = 2
SB_DUMMY = 2
INTERLEAVE = False

ENGS = ("pe", "act", "dve", "pool", "sp")
N_DMA_SEMS = 6


class Res:
    __slots__ = ("w", "r")

    def __init__(self):
        self.w = None
        self.r = {}


class Prog:
    def __init__(self, nc):
        self.nc = nc
        self.streams = {e: [] for e in ENGS}
        self.count = {e: 0 for e in ENGS}
        self.waited = {e: {} for e in ENGS}
        self.dma_cnt = {}
        self.dma_rr = {e: 0 for e in ENGS}

    def _wait(self, eng, dep):
        key, val = dep
        if eng == "pe" and key == "pe":
            return
        if self.waited[eng].get(key, 0) >= val:
            return
        self.waited[eng][key] = val
        self.streams[eng].append(("wait", key, val))

    def _deps(self, eng, reads, writes):
        for r in reads:
            if r.w is not None:
                self._wait(eng, r.w)
        for w in writes:
            if w.w is not None:
                self._wait(eng, w.w)
            for k, v in w.r.items():
                self._wait(eng, (k, v))

    def _mark(self, ev, reads, writes):
        k, v = ev
        for r in reads:
            if r.r.get(k, 0) < v:
                r.r[k] = v
        for w in writes:
            w.w = ev
            w.r = {}

    def op(self, eng, emit, reads=(), writes=()):
        self._deps(eng, reads, writes)
        self.count[eng] += 1
        idx = self.count[eng]
        self.streams[eng].append(("op", emit, idx))
        self._mark((eng, idx), reads, writes)

    def dma(self, q, out, in_, reads=(), writes=()):
        self._deps(q, reads, writes)
        k = self.dma_rr[q]
        self.dma_rr[q] = (k + 1) % N_DMA_SEMS
        key = "dma_%s_%d" % (q, k)
        prev = self.dma_cnt.get(key, 0)
        if prev:
            self._wait(q, (key, prev))
        val = prev + 16
        self.dma_cnt[key] = val
        self.streams[q].append(("dma", out, in_, key))
        self._mark((key, val), reads, writes)

    def handoff(self, src, dst):
        acc = {}
        for r in src:
            if r.w is not None:
                acc[r.w[0]] = max(acc.get(r.w[0], 0), r.w[1])
            for k, v in r.r.items():
                acc[k] = max(acc.get(k, 0), v)
        for d in dst:
            for k, v in acc.items():
                if d.r.get(k, 0) < v:
                    d.r[k] = v

    def final_wait_all(self, eng="sp"):
        for key, val in self.dma_cnt.items():
            self._wait(eng, (key, val))

    def emit(self):
        nc = self.nc
        keys = [e for e in ENGS if self.count[e] > 0] + sorted(self.dma_cnt.keys())
        with contextlib.ExitStack() as st:
            sems = {k: st.enter_context(nc.semaphore("s_" + k)) for k in keys}
            block = st.enter_context(nc.Block())

            def run(eng_name):
                def body(e):
                    for item in self.streams[eng_name]:
                        if item[0] == "wait":
                            e.wait_ge(sems[item[1]], item[2])
                        elif item[0] == "op":
                            item[1](e).then_inc(sems[eng_name], 1)
                        else:
                            _, out, in_, key = item
                            e.dma_start(out=out, in_=in_).then_inc(sems[key], 16)
                return body

            block.sync(run("sp"))
            block.tensor(run("pe"))
            block.scalar(run("act"))
            block.vector(run("dve"))
            block.gpsimd(run("pool"))


class RR:
    def __init__(self, items):
        self.items = items
        self.i = 0

    def get(self):
        it = self.items[self.i]
        self.i = (self.i + 1) % len(self.items)
        return it


def build(n_seq=2, n_layers=2, dbg_stop=None, dbg_y=False):
    nc = bass.Bass("TRN2", target_bir_lowering=False)
    P = Prog(nc)

    def din(name, shape):
        return nc.dram_tensor(name, shape, F32, kind="ExternalInput").ap()

    x_h = din("x", [n_seq, S, D])
    wp_h = din("wp", [DEPTH, 8, 128, NKC * 512])
    wf_h = din("wf", [DEPTH, 128, NKC * 6])
    wo_h = din("wo", [DEPTH, 2, 128, NKC * 512])
    bf_h = din("bfg", [DEPTH, 6, 1])
    gn_h = din("gng", [DEPTH, 1, 384])
    lg_h = din("lng", [DEPTH, 1, D])
    lb_h = din("lnb", [DEPTH, 1, D])
    c128_h = din("c128", [128, 8 * 128])
    oh_h = din("oh", [128, 32])
    msel_h = din("msel", [16, S])
    perm_h = din("perm", [128, 128])
    cos_h = din("cost", [128, S])
    sin_h = din("sint", [128, S])
    iota_h = din("iota", [128, 512])
    dec_h = din("dec", [128, 3 * 10])
    out_h = nc.dram_tensor("out", [n_seq, S, D], F32, kind="ExternalOutput").ap()
    if dbg_y:
        dbgy_h = nc.dram_tensor("dbgy", [128, NKC * S], BF16, kind="ExternalOutput").ap()

    with contextlib.ExitStack() as st:
        def sb(name, shape, dt=BF16):
            return st.enter_context(nc.sbuf_tensor(name, shape, dt))

        def psum(name):
            return st.enter_context(nc.psum_tensor(name, [128, 512], F32))

        xres = sb("xres", [128, NB, D], F32)
        Rxres = [Res() for _ in range(NB)]
        xT = sb("xT", [128, NKC, S])
        RxT = [Res() for _ in range(NT)]
        yT = sb("yT", [128, NKC, S])
        RyT = [[Res() for _ in range(NT)] for _ in range(8)]
        wbuf = [sb("wbuf%d" % i, [128, NKC, 512]) for i in range(2)]
        Rw = [Res() for _ in range(2)]
        wfb2 = sb("wfb", [128, DEPTH, NKC, 6]); Rwf = Res()
        QA = sb("QA", [128, S]); QB = sb("QB", [128, S]); KA = sb("KA", [128, S]); KB = sb("KB", [128, S])
        RQ = [[Res() for _ in range(NT)] for _ in range(2)]
        RK = [[Res() for _ in range(NT)] for _ in range(2)]
        RQaug = [Res(), Res()]
        RKaug = [Res(), Res()]
        Qh = [QA, QB]; Kh = [KA, KB]
        VA = sb("VA", [128, NB, 128]); VB = sb("VB", [128, NB, 128])
        Vh = [VA, VB]
        RV = [[Res() for _ in range(NT)] for _ in range(2)]
        RVones = Res()
        sgA = sb("sgA", [128, S]); sgB = sb("sgB", [128, S])
        sgh = [sgA, sgB]
        Rsg = [[Res() for _ in range(NT)] for _ in range(2)]
        SPall = sb("SPall", [128, NB * 512])
        SPb = [SPall[:, i * 512:(i + 1) * 512] for i in range(NB)]
        RSP = [Res() for _ in range(NB)]
        Ktok = SPall[:, 0:2048].rearrange("p (b c) -> p b c", b=NB); RKtok = [Res() for _ in range(NT)]
        Pt = RR([(sb("Pt%d" % i, [128, 512]), Res()) for i in range(2)])
        f32t = RR([(sb("f32t%d" % i, [128, 512], F32), Res()) for i in range(3)])
        Ebuf = f32t
        rot = [SPall[:, 2048:3072].bitcast(F32), SPall[:, 3072:4096].bitcast(F32)]
        Rrot = Res()
        xb = RR([(sgA[:, 0:1024], Res()), (sgA[:, 1024:2048], Res())])
        xb_res = [xb.items[0][1], xb.items[1][1]]
        csb = RR([(sb("csb%d" % i, [16, 512]), Res()) for i in range(1)])
        negcspT = SPall[0:6, 0:2048]; Rncsp = [Res() for _ in range(NT)]
        csp_tok = sb("csp_tok", [128, NB, 6], F32); Rcsptok = [Res() for _ in range(NT)]
        cs_tmp = RR([(SPall[0:6, 4096 + i * 1024:5120 + i * 1024].bitcast(F32), Res()) for i in range(2)])
        ft1 = RR([(SPall[0:6, 6144 + i * 1024:7168 + i * 1024].bitcast(F32), Res()) for i in range(2)])
        ones6 = SPall[0:6, 2048:3072].bitcast(F32); Rones6 = Res()
        negb = sb("negb", [6, 1], F32); Rnegb = Res()
        braw = sb("braw", [6, 1], F32); Rbraw = Res()
        gng = sb("gng_sb", [128, 384], F32); Rgng = Res()
        lngain = KA[:].bitcast(F32); Rlng = Res()
        lnbias = KB[:].bitcast(F32); Rlnb = Res()
        c128 = sb("c128_sb", [128, 8 * 128]); Rc = Res()
        identf = sb("identf", [6, 8], F32)
        oh = sb("oh_sb", [128, 32])
        iota = SPall[:, 4096:5120].bitcast(F32); Riota = Res()
        dec = sb("dec_sb", [128, 30], F32)
        mask2 = sb("mask2", [128, 256])
        permf = sb("permf", [128, 128], F32)
        Sall = SPall[:, 5120:7168].rearrange("p (b c) -> p b c", b=NB)
        RSall = [Res() for _ in range(NB)]
        st8 = RR([(sb("st8_%d" % i, [128, 8, 6], F32), Res()) for i in range(2)])
        mv8 = RR([(sb("mv8_%d" % i, [128, 8, 2], F32), Res()) for i in range(2)])
        rs8 = RR([(sb("rs8_%d" % i, [128, 8], F32), Res()) for i in range(2)])
        lnst = RR([(sb("lnst%d" % i, [128, 2, 6], F32), Res()) for i in range(4)])
        lnmv = RR([(sb("lnmv%d" % i, [128, 4], F32), Res()) for i in range(4)])
        zt = RR([(QA[:].bitcast(F32), Res()), (QB[:].bitcast(F32), Res()),
                 (VA[:].rearrange("p b c -> p (b c)").bitcast(F32), Res()), (VB[:].rearrange("p b c -> p (b c)").bitcast(F32), Res())])
        qk_res = [r for grp in (RQ, RK, RV) for hh in grp for r in hh] + RQaug + RKaug + [RVones]
        ln_res = [it[1] for it in zt.items] + [Rlng, Rlnb]

        ident = c128[:, 0:128]
        m01 = c128[:, 128:256]
        m01s = c128[:, 256:384]
        negfox = c128[:, 384:512]
        negsb = c128[:, 512:640]
        neguinc = c128[:, 640:768]
        negones = c128[:, 768:896]
        zeros128 = c128[:, 896:1024]

        pbig = RR([(psum("pb%d" % i), Res()) for i in range(3)])
        pobank = RR([(psum("po%d" % i), Res()) for i in range(2)])
        pproj = RR([(psum("pp%d" % i), Res()) for i in range(2)])
        pmisc = RR([(psum("pm%d" % i), Res()) for i in range(1)])
        pln = RR(pproj.items + pbig.items + pobank.items)
        sbbig = RR(pbig.items + pmisc.items)
        ppT = RR(list(pproj.items))
        if not INTERLEAVE:
            pproj = RR(pproj.items + pbig.items)
            pvproj = RR(pmisc.items + pobank.items)
        else:
            pvproj = pmisc

        print("sbuf bytes remaining:", nc.sbuf_bytes_remaining)

        P.dma("pool", c128[:], c128_h, writes=[Rc])
        P.dma("pool", oh[:], oh_h, writes=[Rc])
        P.dma("sp", identf[:], c128_h[0:6, 0:8], writes=[Rc])
        P.dma("sp", dec[:], dec_h, writes=[Rc])
        P.dma("sp", permf[:], perm_h, writes=[Rc])
        P.dma("pool", mask2[:, 0:128], c128_h[:, 128:256], writes=[Rc])
        P.dma("pool", mask2[:, 128:256], c128_h[:, 128:256], writes=[Rc])
        fox_scr = Rncsp + [Rones6] + [it[1] for it in cs_tmp.items] + [it[1] for it in ft1.items]
        ret_scr = RKtok + [Rrot, Riota] + RSall
        sb_scr = RSP

        def ones_init(e):
            e.memset(VA[:, :, 64:128], 1.0)
            return e.memset(VB[:, :, 64:128], 1.0)
        P.op("pool", ones_init, writes=[RVones])
        for l_ in range(DEPTH):
            P.dma("pool", wfb2[:, l_, :, :].rearrange("p k c -> p (k c)"), wf_h[l_], writes=[Rwf])

        wstate = {"i": 0}

        def load_w(src_ap):
            i = wstate["i"]
            wstate["i"] = 1 - i
            for kc in range(NKC):
                P.dma("pool", wbuf[i][:, kc, :], src_ap[:, kc * 512:(kc + 1) * 512], writes=[Rw[i]])
            return wbuf[i], Rw[i]

        def make_xT(tb, evac_eng="dve"):
            P.handoff(Rsg[0] + Rsg[1], xb_res)
            xbt, Rxb = xb.get()
            P.op("act", lambda e: e.activation(out=xbt[:], in_=xres[:, tb, :], func=AF.Copy), reads=[Rxres[tb]], writes=[Rxb])
            pm, Rpm = pmisc.get()
            pmb = pm[:].bitcast(BF16)

            def tr(e):
                ins = None
                for kc in range(NKC):
                    ins = e.transpose(out=pmb[:, kc * 128:(kc + 1) * 128], in_=xbt[:, kc * 128:(kc + 1) * 128], identity=ident)
                return ins
            P.op("pe", tr, reads=[Rxb, Rc], writes=[Rpm])
            if evac_eng == "dve":
                P.op("dve", lambda e: e.tensor_copy(out=xT[:, :, tb * 128:(tb + 1) * 128],
                                                    in_=pmb.rearrange("p (k t) -> p k t", k=NKC)),
                     reads=[Rpm], writes=[RxT[tb // 4]])
            else:
                P.op("act", lambda e: e.activation(out=xT[:, :, tb * 128:(tb + 1) * 128],
                                                   in_=pmb.rearrange("p (k t) -> p k t", k=NKC), func=AF.Copy),
                     reads=[Rpm], writes=[RxT[tb // 4]])
            P.handoff(xb_res, Rsg[0] + Rsg[1])

        def make_xT_stages(tb):
            G = {}

            def cast():
                P.handoff(Rsg[0] + Rsg[1], xb_res)
                xbt, Rxb = xb.get()
                G["xb"] = (xbt, Rxb)
                P.op("act", lambda e: e.activation(out=xbt[:], in_=xres[:, tb, :], func=AF.Copy), reads=[Rxres[tb]], writes=[Rxb])

            def trans():
                xbt, Rxb = G["xb"]
                pm, Rpm = pmisc.get()
                pmb = pm[:].bitcast(BF16)
                G["pm"] = (pmb, Rpm)

                def tr(e):
                    ins = None
                    for kc in range(NKC):
                        ins = e.transpose(out=pmb[:, kc * 128:(kc + 1) * 128], in_=xbt[:, kc * 128:(kc + 1) * 128], identity=ident)
                    return ins
                P.op("pe", tr, reads=[Rxb, Rc], writes=[Rpm])

            def evac():
                pmb, Rpm = G["pm"]
                P.op("act", lambda e: e.activation(out=xT[:, :, tb * 128:(tb + 1) * 128],
                                                   in_=pmb.rearrange("p (k t) -> p k t", k=NKC), func=AF.Copy),
                     reads=[Rpm], writes=[RxT[tb // 4]])
                P.handoff(xb_res, Rsg[0] + Rsg[1])
            return cast, trans, evac

        def proj_fm(wb, Rwb, c0, t):
            pb, Rpb = pproj.get()

            def mm(e):
                ins = None
                for kc in range(NKC):
                    ins = e.matmul(pb[:, :], lhsT=wb[:, kc, c0:c0 + 128], rhs=xT[:, kc, t * 512:(t + 1) * 512],
                                   start=(kc == 0), stop=(kc == NKC - 1))
                return ins
            P.op("pe", mm, reads=[Rwb, RxT[t]], writes=[Rpb])
            return pb, Rpb

        def proj_v(wb, Rwb, t, eng="dve"):
            pb, Rpb = proj_fm(wb, Rwb, 256, t)
            vt, Rvt = Pt.get()
            if eng == "dve":
                P.op("dve", lambda e: e.tensor_copy(out=vt[:], in_=pb[:, :]), reads=[Rpb], writes=[Rvt])
            else:
                P.op("act", lambda e: e.activation(out=vt[:], in_=pb[:, :], func=AF.Copy), reads=[Rpb], writes=[Rvt])
            pm, Rpm = pvproj.get()
            pmb = pm[:].bitcast(BF16)

            def tr(e):
                ins = None
                for j in range(4):
                    ins = e.transpose(out=pmb[:, j * 128:(j + 1) * 128], in_=vt[:, j * 128:(j + 1) * 128], identity=ident)
                return ins
            P.op("pe", tr, reads=[Rvt, Rc], writes=[Rpm])
            pv = pmb[:, 0:512].rearrange("p (j c) -> p j c", j=4)
            if eng == "dve":
                P.op("dve", lambda e: e.tensor_copy(out=VA[:, 4 * t:4 * t + 4, 0:64], in_=pv[:, :, 0:64]),
                     reads=[Rpm, RVones], writes=[RV[0][t]])
                P.op("dve", lambda e: e.tensor_copy(out=VB[:, 4 * t:4 * t + 4, 0:64], in_=pv[:, :, 64:128]),
                     reads=[Rpm, RVones], writes=[RV[1][t]])
            else:
                P.op("act", lambda e: e.activation(out=VA[:, 4 * t:4 * t + 4, 0:64], in_=pv[:, :, 0:64], func=AF.Copy),
                     reads=[Rpm, RVones], writes=[RV[0][t]])
                P.op("act", lambda e: e.activation(out=VB[:, 4 * t:4 * t + 4, 0:64], in_=pv[:, :, 64:128], func=AF.Copy),
                     reads=[Rpm, RVones], writes=[RV[1][t]])

        def fox_prep(layer):
            P.handoff(sb_scr + ret_scr, fox_scr)
            P.op("pool", lambda e: e.memset(ones6[:], 1.0), writes=[Rones6])
            wfb = wfb2[:, layer, :, :]
            P.dma("sp", braw[:], bf_h[layer], writes=[Rbraw])
            P.op("dve", lambda e: e.tensor_scalar(out=negb[:], in0=braw[:], scalar1=-1.0, scalar2=None, op0=ALU.mult),
                 reads=[Rbraw], writes=[Rnegb])
            def kaug(e):
                e.memset(KA[64:65, :], 1.0)
                return e.memset(KB[64:65, :], 1.0)
            P.op("pool", kaug, writes=[RKaug[0], RKaug[1]] + RK[1])
            prev = None
            for t in range(NT):
                pm, Rpm = pmisc.get()

                def mm(e, pm=pm, t=t):
                    ins = None
                    for kc in range(NKC):
                        ins = e.matmul(pm[0:6, :], lhsT=wfb[:, kc, :], rhs=xT[:, kc, t * 512:(t + 1) * 512],
                                       start=(kc == 0), stop=(kc == NKC - 1))
                    return ins
                P.op("pe", mm, reads=[Rwf, RxT[t]], writes=[Rpm])
                f1, Rf1 = ft1.get()
                P.op("act", lambda e, pm=pm, f1=f1: e.activation(out=f1[:], in_=pm[0:6, :], func=AF.Exp, bias=negb[:], scale=-1.0),
                     reads=[Rpm, Rnegb], writes=[Rf1])
                P.op("act", lambda e, f1=f1: e.activation(out=f1[:], in_=f1[:], func=AF.Ln, bias=1.0, scale=1.0),
                     reads=[Rf1], writes=[Rf1])
                cs, Rcs = cs_tmp.get()
                if prev is None:
                    init = 0.0
                    rd = [Rf1, Rones6]
                else:
                    init = prev[0][:, 511:512]
                    rd = [Rf1, Rones6, prev[1]]
                P.op("dve", lambda e, cs=cs, f1=f1, init=init: e.tensor_tensor_scan(
                    out=cs[:], data0=ones6[:], data1=f1[:], initial=init, op0=ALU.mult, op1=ALU.add),
                    reads=rd, writes=[Rcs])
                prev = (cs, Rcs)
                P.op("dve", lambda e, cs=cs, t=t: e.tensor_scalar(out=negcspT[:, t * 512:(t + 1) * 512], in0=cs[:], scalar1=-1.0,
                                                                  scalar2=None, op0=ALU.mult),
                     reads=[Rcs], writes=[Rncsp[t]])
                pm2, Rpm2 = pmisc.get()

                def tr(e, pm2=pm2, cs=cs):
                    ins = None
                    for j in range(4):
                        ins = e.transpose(out=pm2[:, j * 6:(j + 1) * 6], in_=cs[:, j * 128:(j + 1) * 128], identity=identf[0:6, 0:6])
                    return ins
                P.op("pe", tr, reads=[Rcs, Rc], writes=[Rpm2])
                P.op("dve", lambda e, pm2=pm2, t=t: e.tensor_copy(out=csp_tok[:, 4 * t:4 * t + 4, :],
                                                                  in_=pm2[:, 0:24].rearrange("p (j c) -> p j c", j=4)),
                     reads=[Rpm2], writes=[Rcsptok[t]])

        def inproj_headwise(wb, Rwb, kind):
            def setup():
                if kind == "sb":
                    P.handoff(ret_scr + fox_scr, sb_scr)
                    P.dma("pool", KA[64:80, :], msel_h, writes=[RKaug[0]] + RK[1])
                    P.dma("pool", KB[64:80, :], msel_h, writes=[RKaug[1]])

            def q_item(t):
                ts = slice(t * 512, (t + 1) * 512)
                pb, Rpb = proj_fm(wb, Rwb, 0, t)
                P.op("act", lambda e: e.activation(out=QA[0:64, ts], in_=pb[0:64, :], func=AF.Copy, scale=0.125),
                     reads=[Rpb], writes=[RQ[0][t]])
                P.op("dve", lambda e: e.tensor_scalar(out=QB[0:64, ts], in0=pb[64:128, :], scalar1=0.125, scalar2=None, op0=ALU.mult),
                     reads=[Rpb], writes=[RQ[1][t]])

            def k_item(t):
                ts = slice(t * 512, (t + 1) * 512)
                pb, Rpb = proj_fm(wb, Rwb, 128, t)
                P.op("act", lambda e: e.activation(out=KA[0:64, ts], in_=pb[0:64, :], func=AF.Copy), reads=[Rpb], writes=[RK[0][t]])
                P.op("dve", lambda e: e.tensor_copy(out=KB[0:64, ts], in_=pb[64:128, :]), reads=[Rpb], writes=[RK[1][t]])

            def g_item(t):
                ts = slice(t * 512, (t + 1) * 512)
                pb, Rpb = proj_fm(wb, Rwb, 384, t)
                P.op("act", lambda e: e.activation(out=sgA[0:64, ts], in_=pb[0:64, :], func=AF.Silu), reads=[Rpb], writes=[Rsg[0][t]])
                P.op("act", lambda e: e.activation(out=sgB[0:64, ts], in_=pb[64:128, :], func=AF.Silu), reads=[Rpb], writes=[Rsg[1][t]])

            qkv = [[(lambda t=t: q_item(t)), (lambda t=t: k_item(t)), (lambda t=t: proj_v(wb, Rwb, t))] for t in range(NT)]
            gate = [(lambda t=t: g_item(t)) for t in range(NT)]
            return setup, qkv, gate

        def inproj_ret(wb, Rwb, rp):
            def setup():
                P.handoff(sb_scr + fox_scr, ret_scr)
                if rp == 0:
                    P.dma("sp", iota[:], iota_h, writes=[Riota])

            def qk_item(t, which):
                ts = slice(t * 512, (t + 1) * 512)
                c0, dst, Rdst, Raug = ((0, QA, RQ, RQaug[0]), (128, KA, RK, RKaug[0]))[which]
                if which == 0:
                    P.dma("sp", rot[0][:], cos_h[:, ts], writes=[Rrot])
                    P.dma("sp", rot[1][:], sin_h[:, ts], writes=[Rrot])
                pb, Rpb = proj_fm(wb, Rwb, c0, t)
                dt_, Rd = f32t.get()
                col = rp * 10 + which
                bcol = rp * 10 + 2 + which * 4 + t
                P.op("act", lambda e: e.activation(out=dt_[:], in_=iota[:], func=AF.Exp, bias=dec[:, bcol:bcol + 1],
                                                   scale=dec[:, col:col + 1]),
                     reads=[Rc, Riota], writes=[Rd])
                qd, Rqd = f32t.get()
                P.op("dve", lambda e: e.tensor_tensor(out=qd[:], in0=pb[:, :], in1=dt_[:], op=ALU.mult), reads=[Rpb, Rd], writes=[Rqd])
                u, Ru = f32t.get()
                psw, Rpsw = ppT.get()
                P.op("pe", lambda e: e.matmul(psw[:, :], lhsT=permf[:], rhs=qd[:], start=True, stop=True),
                     reads=[Rqd, Rc], writes=[Rpsw])
                P.op("dve", lambda e: e.tensor_tensor(out=u[:], in0=psw[:, :], in1=rot[1][:], op=ALU.mult),
                     reads=[Rpsw, Rrot], writes=[Ru])
                P.op("dve", lambda e: e.tensor_tensor(out=qd[:], in0=qd[:], in1=rot[0][:], op=ALU.mult),
                     reads=[Rqd, Rrot, Ru], writes=[Rqd])
                P.op("dve", lambda e: e.tensor_tensor(out=dst[:, ts], in0=qd[:], in1=u[:], op=ALU.add),
                     reads=[Rqd, Ru], writes=[Rdst[0][t], Rdst[1][t], Raug])

            def g_item(t):
                ts = slice(t * 512, (t + 1) * 512)
                pb, Rpb = proj_fm(wb, Rwb, 384, t)
                P.op("act", lambda e: e.activation(out=sgA[:, ts], in_=pb[:, :], func=AF.Silu), reads=[Rpb], writes=[Rsg[0][t], Rsg[1][t]])

            qkv = [[(lambda t=t: qk_item(t, 0)), (lambda t=t: qk_item(t, 1)), (lambda t=t: proj_v(wb, Rwb, t, "act"))] for t in range(NT)]
            gate = [(lambda t=t: g_item(t)) for t in range(NT)]
            return setup, qkv, gate

        def flatten(items, depth):
            out = []
            n = len(items)
            for i in range(n + depth):
                if i < n:
                    out.append(items[i][0])
                if i >= depth:
                    out.append(items[i - depth][1])
            return out

        def run_merged(main, fill):
            n, m = len(main), len(fill)
            if m == 0:
                for f in main:
                    f()
                return
            stride = max(1, n // (m + 1))
            k = 0
            for i, f in enumerate(main):
                f()
                if k < m and (i + 1) % stride == 0:
                    fill[k]()
                    k += 1
            while k < m:
                fill[k]()
                k += 1

        def pipeline(items, depth):
            n = len(items)
            for i in range(n + depth):
                if i < n:
                    items[i][0]()
                if i >= depth:
                    items[i - depth][1]()

        def fox_items(hi, gh, hp, t):
            Q, K, V, sg = Qh[hi], Kh[hi], Vh[hi], sgh[hi]
            r0 = hi * 64
            items = []
            if True:
                nkb = 4 * (t + 1)
                tstate = {}
                for kb in range(nkb):
                    j = kb - 4 * t
                    c0 = max(j, 0) * 128
                    ks = slice(kb * 128, (kb + 1) * 128)
                    st_ = {}

                    def A(st_=st_, ks=ks, c0=c0, j=j, t=t, kb=kb):
                        ps, Rps = pbig.get()
                        st_["ps"] = (ps, Rps)

                        def mm(e):
                            ins = e.matmul(ps[:, c0:512], lhsT=K[0:65, ks], rhs=Q[0:65, t * 512 + c0:(t + 1) * 512],
                                           start=True, stop=(j < 0))
                            if j >= 0:
                                ins = e.matmul(ps[:, c0:c0 + 128], lhsT=ident, rhs=negfox, start=False, stop=True,
                                               skip_group_check=True)
                            return ins
                        P.op("pe", mm, reads=[RK[hi][kb // 4], RKaug[hi], RQ[hi][t], RQaug[hi], Rc], writes=[Rps])

                    def B(st_=st_, tstate=tstate, c0=c0, kb=kb, nkb=nkb, t=t):
                        ps, Rps = st_["ps"]
                        if kb == 0:
                            tstate["po"] = pobank.get()
                        po, Rpo = tstate["po"]
                        pt, Rpt = Pt.get()
                        P.op("act", lambda e: e.activation(out=pt[:, c0:512], in_=ps[:, c0:512], func=AF.Exp,
                                                           bias=csp_tok[:, kb, gh:gh + 1], scale=1.0),
                             reads=[Rps, Rcsptok[kb // 4]], writes=[Rpt])
                        P.op("pe", lambda e: e.matmul(po[:, c0:512], lhsT=V[:, kb, :], rhs=pt[:, c0:512], start=(kb == 0),
                                                      stop=(kb == nkb - 1), skip_group_check=True),
                             reads=[Rpt, RV[hi][kb // 4], RVones], writes=[Rpo])
                        if kb == nkb - 1:
                            ts = slice(t * 512, (t + 1) * 512)
                            rc, Rrc = f32t.get()
                            P.op("dve", lambda e: e.reciprocal(out=rc[0:64, :], in_=po[64:128, :]), reads=[Rpo], writes=[Rrc])
                            P.op("dve", lambda e: e.tensor_tensor(out=rc[0:64, :], in0=rc[0:64, :], in1=sg[0:64, ts], op=ALU.mult),
                                 reads=[Rrc, Rsg[hi][t]], writes=[Rrc])
                            P.op("dve", lambda e: e.tensor_tensor(out=yT[r0:r0 + 64, hp, ts], in0=po[0:64, :], in1=rc[0:64, :],
                                                                  op=ALU.mult),
                                 reads=[Rpo, Rrc], writes=[RyT[hp][t]])
                    items.append((A, B))
            return items

        def sb_stream(hi, hp, t, raw=False):
            Q, K, V, sg = Qh[hi], Kh[hi], Vh[hi], sgh[hi]
            r0 = hi * 64
            stream = []
            pcs_box = {}
            if True:
                nkb = 4 * (t + 1)
                items = []
                for kb in range(nkb):
                    j = kb - 4 * t
                    c0 = max(j, 0) * 128
                    r = nkb - 1 - kb
                    ks = slice(kb * 128, (kb + 1) * 128)
                    st_ = {}

                    def A(st_=st_, ks=ks, c0=c0, t=t, kb=kb):
                        pz, Rpz = sbbig.get()
                        st_["pz"] = (pz, Rpz)
                        P.op("pe", lambda e: e.matmul(pz[:, c0:512], lhsT=K[0:64, ks], rhs=Q[0:64, t * 512 + c0:(t + 1) * 512],
                                                      start=True, stop=True),
                             reads=[RK[hi][kb // 4], RQ[hi][t]], writes=[Rpz])

                    def B(st_=st_, c0=c0, kb=kb, j=j, r=r, nkb=nkb):
                        pz, Rpz = st_["pz"]
                        if kb == 0:
                            pcs_box["p"] = pobank.get()
                        pcs, Rpcs = pcs_box["p"]
                        eb, Reb = Ebuf.get()
                        P.op("act", lambda e: e.activation(out=eb[:, c0:512], in_=pz[:, c0:512], func=AF.Exp),
                             reads=[Rpz], writes=[Reb])
                        P.op("act", lambda e: e.activation(out=SPb[kb][:, c0:512], in_=eb[:, c0:512], func=AF.Ln, bias=1.0, scale=1.0),
                             reads=[Reb], writes=[RSP[kb]])
                        if j >= 0:
                            P.op("dve", lambda e: e.tensor_tensor(out=SPb[kb][:, c0:c0 + 128], in0=SPb[kb][:, c0:c0 + 128],
                                                                   in1=m01s, op=ALU.mult),
                                 reads=[RSP[kb], Rc], writes=[RSP[kb]])
                        P.op("pe", lambda e: e.matmul(pcs[0:16, c0:512], lhsT=oh[:, 15 - kb:31 - kb], rhs=SPb[kb][:, c0:512],
                                                      start=(kb == 0), stop=(kb == nkb - 1), skip_group_check=True),
                             reads=[RSP[kb], Rc], writes=[Rpcs])
                    items.append((A, B))
                p1_items = items
                stream += flatten(items, 2)

                def cscopy():
                    pcs, Rpcs = pcs_box["p"]
                    P.op("dve", lambda e: e.tensor_copy(out=Q[64:80, t * 512:(t + 1) * 512], in_=pcs[0:16, :]),
                         reads=[Rpcs], writes=[RQaug[hi]])
                stream.append(cscopy)
                tstate = {}
                items = []
                for kb in range(nkb):
                    j = kb - 4 * t
                    c0 = max(j, 0) * 128
                    r = nkb - 1 - kb
                    ks = slice(kb * 128, (kb + 1) * 128)
                    st_ = {}

                    def A2(st_=st_, ks=ks, c0=c0, j=j, r=r, t=t, kb=kb):
                        pa, Rpa = sbbig.get()
                        st_["pa"] = (pa, Rpa)

                        def mm(e):
                            e.matmul(pa[:, c0:512], lhsT=K[0:80, ks], rhs=Q[0:80, t * 512 + c0:(t + 1) * 512], start=True, stop=False)
                            ins = e.matmul(pa[:, c0:512], lhsT=neguinc, rhs=SPb[kb][:, c0:512], start=False, stop=(j < 0),
                                           skip_group_check=True)
                            if j >= 0:
                                ins = e.matmul(pa[:, c0:c0 + 128], lhsT=ident, rhs=negsb, start=False, stop=True, skip_group_check=True)
                            return ins
                        P.op("pe", mm, reads=[RK[hi][kb // 4], RKaug[hi], RQ[hi][t], RQaug[hi], RSP[kb], Rc], writes=[Rpa])

                    def B2(st_=st_, tstate=tstate, c0=c0, kb=kb, nkb=nkb, t=t):
                        pa, Rpa = st_["pa"]
                        if kb == 0:
                            tstate["po"] = pobank.get()
                        po, Rpo = tstate["po"]
                        pt, Rpt = Pt.get()
                        P.op("act", lambda e: e.activation(out=pt[:, c0:512], in_=pa[:, c0:512], func=AF.Exp),
                             reads=[Rpa], writes=[Rpt])
                        P.op("pe", lambda e: e.matmul(po[:, c0:512], lhsT=V[:, kb, :], rhs=pt[:, c0:512], start=(kb == 0),
                                                      stop=(kb == nkb - 1), skip_group_check=True),
                             reads=[Rpt, RV[hi][kb // 4], RVones], writes=[Rpo])
                        if kb == nkb - 1:
                            ts = slice(t * 512, (t + 1) * 512)
                            P.op("dve", lambda e: e.tensor_tensor(out=yT[r0:r0 + 64, hp, ts], in0=po[0:64, :], in1=sg[0:64, ts],
                                                                  op=ALU.mult),
                                 reads=[Rpo, Rsg[hi][t]], writes=[RyT[hp][t]])
                    items.append((A2, B2))
                stream += flatten(items, 2)
            if raw:
                return p1_items, cscopy, items
            return stream

        def sb_pair(hp):
            units = [sb_stream(hi, hp, t, raw=True) for t in range(NT) for hi in range(2)]
            pipeline(units[0][0], 2)
            units[0][1]()
            for i in range(len(units)):
                p2 = units[i][2]
                p1n = units[i + 1][0] if i + 1 < len(units) else []
                n = max(len(p2), len(p1n))
                for k in range(n + 1):
                    if k < len(p2):
                        p2[k][0]()
                    if k < len(p1n):
                        p1n[k][0]()
                    if SB_DUMMY:
                        jb, Rjb = pproj.items[0]

                        def dmm(e, jb=jb):
                            ins = None
                            for _ in range(SB_DUMMY):
                                ins = e.matmul(jb[:, :], lhsT=ident, rhs=c128[:, 0:512], start=True, stop=True)
                            return ins
                        P.op("pe", dmm, reads=[Rc], writes=[Rjb])
                    if 0 <= k - 1 < len(p2):
                        p2[k - 1][1]()
                    if 0 <= k - 1 < len(p1n):
                        p1n[k - 1][1]()
                if i + 1 < len(units):
                    units[i + 1][1]()

        def ret_stream(rp, hp, tg, gstate, raw=False):
            def ktok():
                pm, Rpm = ppT.get()
                pmb = pm[:].bitcast(BF16)

                def tr(e):
                    ins = None
                    for jj in range(4):
                        c = 4 * tg + jj
                        ins = e.transpose(out=pmb[:, jj * 128:(jj + 1) * 128], in_=KA[:, c * 128:(c + 1) * 128], identity=ident)
                    return ins
                P.op("pe", tr, reads=[RK[0][tg], RK[1][tg], Rc], writes=[Rpm])
                P.op("act", lambda e: e.activation(out=Ktok[:, 4 * tg:4 * tg + 4, :],
                                                   in_=pmb[:, 0:512].rearrange("p (j c) -> p j c", j=4), func=AF.Copy),
                     reads=[Rpm], writes=[RKtok[tg]])
            items = [ret_chunk_item(rp, hp, c, gstate) for c in range(4 * tg, 4 * tg + 4)]
            if raw:
                return ktok, items
            return [ktok] + flatten(items, 1)

        def ret_chunk_item(rp, hp, c, gstate):
            cg, ci = c // 4, c % 4
            cs_ = slice(c * 128, (c + 1) * 128)
            st_ = {}

            def A():
                pst, Rpst = pbig.get()
                st_["p"] = (pst, Rpst, pst, Rpst)

                pS, RpS = pmisc.items[0]

                def mm(e):
                    e.matmul(pst[:, 0:128], lhsT=KA[0:64, cs_], rhs=QA[0:64, cs_], start=True, stop=True)
                    if c == 0:
                        e.matmul(pS[:, 0:128], lhsT=zeros128, rhs=c128[:, 0:128], start=True, stop=False, skip_group_check=True)
                    e.matmul(pS[:, 0:64], lhsT=Ktok[:, c, :], rhs=VA[:, c, 0:64], start=False, stop=(c == NB - 1),
                             skip_group_check=True)
                    e.matmul(pS[:, 64:128], lhsT=Ktok[:, c, :], rhs=VB[:, c, 0:64], start=False, stop=(c == NB - 1),
                             skip_group_check=True)
                    return e.matmul(pst[:, 128:256], lhsT=KA[64:128, cs_], rhs=QA[64:128, cs_], start=True, stop=True)
                P.op("pe", mm, reads=[RK[0][cg], RK[1][cg], RQ[0][cg], RQ[1][cg], RKtok[cg], RV[0][cg], RV[1][cg], Rc],
                     writes=[Rpst, RpS])
                if c < NB - 1:
                    P.op("act", lambda e: e.activation(out=Sall[:, c + 1, :], in_=pS[:, 0:128], func=AF.Copy),
                         reads=[RpS], writes=[RSall[c + 1]])

            def B():
                pst, Rpst, pst2, Rpst2 = st_["p"]
                if ci == 0:
                    gstate["po"] = pobank.get()
                po, Rpo = gstate["po"]
                pt, Rpt = Pt.get()

                P.op("dve", lambda e: e.tensor_tensor(out=pt[:, 0:256], in0=pst[:, 0:256], in1=mask2[:], op=ALU.mult),
                     reads=[Rpst, Rc], writes=[Rpt])

                def om(e):
                    o0 = ci * 128
                    e.matmul(po[:, o0:o0 + 64], lhsT=pt[:, 0:128], rhs=VA[:, c, 0:64], start=True, stop=(c == 0))
                    if c > 0:
                        e.matmul(po[:, o0:o0 + 64], lhsT=QA[0:64, cs_], rhs=Sall[0:64, c, 0:64], start=False, stop=True)
                    ins = e.matmul(po[:, o0 + 64:o0 + 128], lhsT=pt[:, 128:256], rhs=VB[:, c, 0:64], start=True, stop=(c == 0))
                    if c > 0:
                        ins = e.matmul(po[:, o0 + 64:o0 + 128], lhsT=QA[64:128, cs_], rhs=Sall[64:128, c, 64:128], start=False, stop=True)
                    return ins
                P.op("pe", om, reads=[Rpt, RV[0][cg], RV[1][cg], RQ[0][cg], RQ[1][cg], RSall[c]], writes=[Rpo])
                pend = gstate.setdefault("pend", [])
                for ent in list(pend):
                    ent[0] -= 1
                    if ent[0] <= 0:
                        pend.remove(ent)
                        ent[1]()
                if ci == 3:
                    st1, st2, st3 = ret_gn(rp, hp, cg, po, Rpo)
                    st1()
                    pend.append([1, st2])
                    pend.append([2, st3])
            return (A, B)

        def ret_gn(rp, hp, cg, po, Rpo):
            G = {}

            def stage1():
                if True:
                    s8, Rs8 = st8.get()
                    pov = po[:].rearrange("p (g d) -> p g d", g=8)
                    def gstats(e, s8=s8, pov=pov):
                        ins = None
                        for g in range(8):
                            ins = e.bn_stats(out=s8[:, g, :], in_=pov[:, g, :])
                        return ins
                    P.op("dve", gstats, reads=[Rpo], writes=[Rs8])
                    m8, Rm8 = mv8.get()
                    def aggr(e, s8=s8, m8=m8):
                        ins = None
                        for g in range(8):
                            ins = e.bn_aggr(out=m8[:, g, :], in_=s8[:, g, :])
                        return ins
                    P.op("dve", aggr, reads=[Rs8], writes=[Rm8])
                    r8, Rr8 = rs8.get()
                    P.op("act", lambda e, r8=r8, m8=m8: e.activation(out=r8[:], in_=m8[:, :, 1], func=AF.Ln, bias=GN_EPS, scale=1.0),
                         reads=[Rm8], writes=[Rr8])
                    P.op("act", lambda e, r8=r8: e.activation(out=r8[:], in_=r8[:], func=AF.Exp, scale=-0.5),
                         reads=[Rr8], writes=[Rr8])
                    G.update(pov=pov, m8=m8, Rm8=Rm8, r8=r8, Rr8=Rr8)

            def stage2():
                if True:
                    pov, m8, Rm8, r8, Rr8 = G["pov"], G["m8"], G["Rm8"], G["r8"], G["Rr8"]
                    t1, Rt1 = f32t.get()
                    t1v = t1[:].rearrange("p (g d) -> p g d", g=8)

                    def gnorm(e, t1v=t1v, pov=pov, m8=m8, r8=r8):
                        ins = None
                        for g in range(8):
                            ins = e.tensor_scalar(out=t1v[:, g, :], in0=pov[:, g, :], scalar1=m8[:, g, 0:1], scalar2=r8[:, g:g + 1],
                                                  op0=ALU.subtract, op1=ALU.mult)
                        return ins
                    P.op("dve", gnorm, reads=[Rpo, Rm8, Rr8], writes=[Rt1])
                    pt2, Rpt2 = Pt.get()

                    def ggain(e, pt2=pt2, t1=t1):
                        ins = None
                        for cc in range(4):
                            ins = e.tensor_tensor(out=pt2[:, cc * 128:(cc + 1) * 128], in0=t1[:, cc * 128:(cc + 1) * 128],
                                                  in1=gng[:, rp * 128:(rp + 1) * 128], op=ALU.mult)
                        return ins
                    P.op("dve", ggain, reads=[Rt1, Rgng], writes=[Rpt2])
                    pm, Rpm = ppT.get()
                    pmb = pm[:].bitcast(BF16)

                    def tr(e, pmb=pmb, pt2=pt2):
                        ins = None
                        for jj in range(4):
                            ins = e.transpose(out=pmb[:, jj * 128:(jj + 1) * 128], in_=pt2[:, jj * 128:(jj + 1) * 128], identity=ident)
                        return ins
                    P.op("pe", tr, reads=[Rpt2, Rc], writes=[Rpm])
                    G.update(pmb=pmb, Rpm=Rpm)

            def stage3():
                if True:
                    pmb, Rpm = G["pmb"], G["Rpm"]
                    ts = slice(cg * 512, (cg + 1) * 512)
                    P.op("dve", lambda e, pmb=pmb, ts=ts: e.tensor_tensor(out=yT[:, hp, ts], in0=pmb[:, 0:512], in1=sgA[:, ts],
                                                                          op=ALU.mult),
                         reads=[Rpm, Rsg[0][cg], Rsg[1][cg]], writes=[RyT[hp][cg]])
            return [stage1, stage2, stage3]

        def ln_item(s, tb, last, w0, Rw0, w1, Rw1):
            tks = slice(tb * 128, (tb + 1) * 128)
            st_ = {}

            def S0():
                pbs = []
                for half, (wb, Rwb) in enumerate(((w0, Rw0), (w1, Rw1))):
                    pb, Rpb = pln.get()
                    pbs.append((pb, Rpb))

                    def mm(e, pb=pb, wb=wb):
                        ins = None
                        for ec in range(NKC):
                            ins = e.matmul(pb[:, :], lhsT=yT[:, ec, tks], rhs=wb[:, ec, :], start=(ec == 0), stop=(ec == NKC - 1))
                        return ins
                    P.op("pe", mm, reads=[Rwb] + [RyT[c][tb // 4] for c in range(8)], writes=[Rpb])
                st_["pbs"] = pbs

            def S1():
                z, Rz = zt.get()
                for half, (pb, Rpb) in enumerate(st_["pbs"]):
                    hs = slice(half * 512, (half + 1) * 512)
                    P.op("dve", lambda e, pb=pb, hs=hs: e.scalar_tensor_tensor(
                        out=z[:, hs], in0=xres[:, tb, hs], scalar=ALPHA, in1=pb[:, :], op0=ALU.mult, op1=ALU.add),
                        reads=[Rpb, Rxres[tb]], writes=[Rz])
                ls, Rls = lnst.get()

                def lnstats(e):
                    e.bn_stats(out=ls[:, 0, :], in_=z[:, 0:512])
                    return e.bn_stats(out=ls[:, 1, :], in_=z[:, 512:1024])
                P.op("dve", lnstats, reads=[Rz], writes=[Rls])
                lm, Rlm = lnmv.get()
                P.op("dve", lambda e: e.bn_aggr(out=lm[:, 0:2], in_=ls[:].rearrange("p g s -> p (g s)")), reads=[Rls], writes=[Rlm])
                st_["z"] = (z, Rz, lm, Rlm)

            def S2():
                z, Rz, lm, Rlm = st_["z"]
                P.op("act", lambda e: e.activation(out=lm[:, 2:3], in_=lm[:, 1:2], func=AF.Ln, bias=LN_EPS, scale=1.0),
                     reads=[Rlm], writes=[Rlm])
                P.op("act", lambda e: e.activation(out=lm[:, 2:3], in_=lm[:, 2:3], func=AF.Exp, scale=-0.5),
                     reads=[Rlm], writes=[Rlm])

            def S3():
                z, Rz, lm, Rlm = st_["z"]
                P.op("dve", lambda e: e.scalar_tensor_tensor(out=z[:], in0=z[:], scalar=lm[:, 0:1], in1=lngain[:],
                                                             op0=ALU.subtract, op1=ALU.mult),
                     reads=[Rz, Rlm, Rlng], writes=[Rz])
                P.op("dve", lambda e: e.scalar_tensor_tensor(out=xres[:, tb, :], in0=z[:], scalar=lm[:, 2:3], in1=lnbias[:],
                                                             op0=ALU.mult, op1=ALU.add),
                     reads=[Rz, Rlm, Rlnb], writes=[Rxres[tb]])

            if last:
                def S4():
                    P.dma("sp", out_h[s, tks, :], xres[:, tb, :], reads=[Rxres[tb]])
                nop = lambda: None
                return [S0, S1, S2, S3, S4, nop, nop]
            cast, trans, evac = make_xT_stages(tb)
            return [S0, S1, S2, S3, cast, trans, evac]

        def outproj_ln(layer, s, last):
            w0, Rw0 = load_w(wo_h[layer, 0])
            w1, Rw1 = load_w(wo_h[layer, 1])
            P.handoff(qk_res, ln_res)
            P.dma("sp", lngain[:], lg_h[layer].broadcast_to([128, D]), writes=[Rlng])
            P.dma("sp", lnbias[:], lb_h[layer].broadcast_to([128, D]), writes=[Rlnb])
            items = []
            for tb in range(NB):
                items.append(ln_item(s, tb, last, w0, Rw0, w1, Rw1))
            n = len(items)
            order = [0, 1, 2, 3, 4, 6, 5]
            for i in range(n + 6):
                for sidx in order:
                    if 0 <= i - sidx < n:
                        items[i - sidx][sidx]()
            P.handoff(ln_res, qk_res)
            P.op("pool", ones_init, writes=[RVones])

        def main_program(stage):
            for s in range(n_seq):
                for tb in range(NB):
                    P.dma("sp", xres[:, tb, :], x_h[s, tb * 128:(tb + 1) * 128, :], writes=[Rxres[tb]])
                for tb in range(NB):
                    make_xT(tb)
                for layer in range(n_layers):
                    P.dma("sp", gng[:], gn_h[layer].broadcast_to([128, 384]), writes=[Rgng])
                    nxt = load_w(wp_h[layer, 0])
                    fox_prep(layer)
                    stage("fox_prep")
                    for hp in range(8):
                        wb, Rwb = nxt
                        if hp < 7:
                            nxt = load_w(wp_h[layer, hp + 1])
                        if hp < 3:
                            setup, qkv, gate = inproj_headwise(wb, Rwb, "fox")
                        elif hp < 6:
                            setup, qkv, gate = inproj_ret(wb, Rwb, hp - 3)
                        else:
                            setup, qkv, gate = inproj_headwise(wb, Rwb, "sb")
                        setup()
                        if INTERLEAVE:
                            for f in qkv[0] + gate:
                                f()
                        else:
                            for t in range(NT):
                                for f in qkv[t]:
                                    f()
                            for f in gate:
                                f()
                        if hp < 3:
                            for hi in range(2):
                                gh = 2 * hp + hi
                                P.dma("sp", Qh[hi][64:65, :], negcspT[gh:gh + 1, :], reads=Rncsp, writes=[RQaug[hi]] + RQ[1])
                        gstate = {}
                        if not INTERLEAVE and hp < 3:
                            items = []
                            for t in range(NT):
                                items += fox_items(0, 2 * hp, hp, t) + fox_items(1, 2 * hp + 1, hp, t)
                            pipeline(items, 2)
                        elif not INTERLEAVE and hp < 6:
                            pre, items = [], []
                            for t in range(NT):
                                st = ret_stream(hp - 3, hp, t, gstate, raw=True)
                                pre.append(st[0])
                                items += st[1]
                            for f in pre:
                                f()
                            pipeline(items, RET_DEPTH)
                            for ent in sorted(gstate.get("pend", []), key=lambda x: x[0]):
                                ent[1]()
                        if not INTERLEAVE and hp >= 6:
                            sb_pair(hp)
                        for t in range(NT):
                            if not INTERLEAVE:
                                break
                            if hp < 3:
                                main = flatten(fox_items(0, 2 * hp, hp, t) + fox_items(1, 2 * hp + 1, hp, t), 2)
                            elif hp < 6:
                                main = ret_stream(hp - 3, hp, t, gstate)
                            else:
                                main = sb_stream(0, hp, t) + sb_stream(1, hp, t)
                            run_merged(main, qkv[t + 1] if (INTERLEAVE and t + 1 < NT) else [])
                        stage("pair %d" % hp)
                    if dbg_y and s == 0 and layer == 0:
                        P.dma("sp", dbgy_h, yT[:].rearrange("p k t -> p (k t)"), reads=[RyT[c][t] for c in range(8) for t in range(NT)])
                    outproj_ln(layer, s, last=(layer == n_layers - 1))

        class StopBuild(Exception):
            pass
        stg = {"n": 0}

        def stage(name):
            stg["n"] += 1
            if dbg_stop is not None and stg["n"] == dbg_stop:
                print("debug stop at stage", stg["n"], name)
                raise StopBuild()

        try:
            main_program(stage)
        except StopBuild:
            for tb in range(NB):
                P.dma("sp", out_h[0, tb * 128:(tb + 1) * 128, :], xres[:, tb, :], reads=[Rxres[tb]])
        P.final_wait_all("sp")
        P.emit()
    return nc


def host_consts():
    k = np.arange(128)[:, None]
    q = np.arange(128)[None, :]
    ident = (k == q).astype(np.float32)
    m01 = (k <= q).astype(np.float32)
    m01s = (k < q).astype(np.float32)
    negfox = np.where(k > q, NEG, 0.0).astype(np.float32)
    negsb = np.where(k >= q, NEG, 0.0).astype(np.float32)
    neguinc = np.where(k >= q, -1.0, 0.0).astype(np.float32)
    negones = -np.ones((128, 128), np.float32)
    c128 = np.concatenate([ident, m01, m01s, negfox, negsb, neguinc, negones, np.zeros((128, 128), np.float32)], axis=1)
    oh = np.zeros((128, 32), np.float32)
    oh[:, 15] = 1.0
    perm = np.zeros((128, 128), np.float32)
    for p in range(128):
        perm[p + 32 if (p % 64) < 32 else p - 32, p] = 1.0
    msel = np.where(np.arange(16)[:, None] > (np.arange(S)[None, :] // 128), -1.0, 0.0).astype(np.float32)
    half = 32
    inv_freq = (1.0 / (10000.0 ** (np.arange(half, dtype=np.float32) / half))).astype(np.float32)
    pos = np.arange(S, dtype=np.float32)
    ang = pos[None, :] * inv_freq[:, None]
    cos32 = np.cos(ang).astype(np.float32)
    sin32 = np.sin(ang).astype(np.float32)
    cost = np.tile(cos32, (4, 1))
    sint = np.concatenate([-sin32, sin32, -sin32, sin32], axis=0)
    iota = np.tile(np.arange(512, dtype=np.float32)[None, :], (128, 1))
    log_g = np.log(1.0 - 2.0 ** (-5.0 - np.arange(6, dtype=np.float64)))
    dec = np.zeros((128, 30), np.float32)
    for rp in range(3):
        for p in range(128):
            lg = log_g[2 * rp + (1 if p >= 64 else 0)]
            dec[p, rp * 10 + 0] = lg
            dec[p, rp * 10 + 1] = -lg
            for t in range(4):
                dec[p, rp * 10 + 2 + t] = lg * 512 * t
                dec[p, rp * 10 + 6 + t] = -lg * 512 * t + math.log(0.125)
    return dict(c128=c128, oh=oh, msel=msel, perm=perm, cost=cost, sint=sint, iota=iota, dec=dec)


def host_weights(w_in, w_out, b_fgate, ret_gn_gain, ln_gain, ln_bias):
    L = w_in.shape[0]
    w4 = np.ascontiguousarray(w_in[:, :, :4096]).reshape(L, NKC, 128, 4, 8, 128)
    wp = np.ascontiguousarray(w4.transpose(0, 4, 2, 1, 3, 5)).reshape(L, 8, 128, NKC * 512)
    wf = np.ascontiguousarray(w_in[:, :, 4096:4102].reshape(L, NKC, 128, 6).transpose(0, 2, 1, 3)).reshape(L, 128, NKC * 6)
    wo4 = w_out.reshape(L, NKC, 128, 2, 512)
    wo = np.ascontiguousarray(wo4.transpose(0, 3, 2, 1, 4)).reshape(L, 2, 128, NKC * 512)
    return dict(wp=wp, wf=wf, wo=wo,
                bfg=np.ascontiguousarray(b_fgate.reshape(L, 6, 1)),
                gng=np.ascontiguousarray(ret_gn_gain.reshape(L, 1, 384)),
                lng=np.ascontiguousarray(ln_gain.reshape(L, 1, D)),
                lnb=np.ascontiguousarray(ln_bias.reshape(L, 1, D)))


_NC_CACHE = {}


def kernel(x, w_in, b_fgate, ret_gn_gain, w_out, ln_gain, ln_bias):
    x = np.asarray(x, np.float32)
    n_cores = 8
    n_seq = x.shape[0] // n_cores
    shared = host_consts()
    shared.update(host_weights(np.asarray(w_in, np.float32), np.asarray(w_out, np.float32), np.asarray(b_fgate, np.float32),
                               np.asarray(ret_gn_gain, np.float32), np.asarray(ln_gain, np.float32), np.asarray(ln_bias, np.float32)))
    if "nc" not in _NC_CACHE:
        _NC_CACHE["nc"] = build(n_seq, DEPTH)
    nc = _NC_CACHE["nc"]
    in_maps = []
    for c in range(n_cores):
        m = dict(shared)
        m["x"] = np.ascontiguousarray(x[c * n_seq:(c + 1) * n_seq])
        in_maps.append(m)
    res = run_bass_kernel_spmd(nc, in_maps, core_ids=list(range(n_cores)))
    return np.concatenate([r["out"] for r in res.results], axis=0).astype(np.float32)
```

```python
import contextlib
import math
import numpy as np
import concourse.bass as bass
import concourse.mybir as mybir
from concourse.bass_utils import run_bass_kernel_spmd

F32 = mybir.dt.float32
BF16 = mybir.dt.bfloat16
AF = mybir.ActivationFunctionType
ALU = mybir.AluOpType

S = 2048
D = 1024
NB = 16
NT = 4
NKC = 8
DEPTH = 2
ALPHA = (2 * DEPTH) ** 0.25
LN_EPS = 1e-5
GN_EPS = 1e-5
NEG = -30000.0
RET_DEPTH = 2
SB_DUMMY = 2
INTERLEAVE = False

ENGS = ("pe", "act", "dve", "pool", "sp")
N_DMA_SEMS = 6


class Res:
    __slots__ = ("w", "r")

    def __init__(self):
        self.w = None
        self.r = {}


class Prog:
    def __init__(self, nc):
        self.nc = nc
        self.streams = {e: [] for e in ENGS}
        self.count = {e: 0 for e in ENGS}
        self.waited = {e: {} for e in ENGS}
        self.dma_cnt = {}
        self.dma_rr = {e: 0 for e in ENGS}

    def _wait(self, eng, dep):
        key, val = dep
        if eng == "pe" and key == "pe":
            return
        if self.waited[eng].get(key, 0) >= val:
            return
        self.waited[eng][key] = val
        self.streams[eng].append(("wait", key, val))

    def _deps(self, eng, reads, writes):
        for r in reads:
            if r.w is not None:
                self._wait(eng, r.w)
        for w in writes:
            if w.w is not None:
                self._wait(eng, w.w)
            for k, v in w.r.items():
                self._wait(eng, (k, v))

    def _mark(self, ev, reads, writes):
        k, v = ev
        for r in reads:
            if r.r.get(k, 0) < v:
                r.r[k] = v
        for w in writes:
            w.w = ev
            w.r = {}

    def op(self, eng, emit, reads=(), writes=()):
        self._deps(eng, reads, writes)
        self.count[eng] += 1
        idx = self.count[eng]
        self.streams[eng].append(("op", emit, idx))
        self._mark((eng, idx), reads, writes)

    def dma(self, q, out, in_, reads=(), writes=()):
        self._deps(q, reads, writes)
        k = self.dma_rr[q]
        self.dma_rr[q] = (k + 1) % N_DMA_SEMS
        key = "dma_%s_%d" % (q, k)
        prev = self.dma_cnt.get(key, 0)
        if prev:
            self._wait(q, (key, prev))
        val = prev + 16
        self.dma_cnt[key] = val
        self.streams[q].append(("dma", out, in_, key))
        self._mark((key, val), reads, writes)

    def handoff(self, src, dst):
        acc = {}
        for r in src:
            if r.w is not None:
                acc[r.w[0]] = max(acc.get(r.w[0], 0), r.w[1])
            for k, v in r.r.items():
                acc[k] = max(acc.get(k, 0), v)
        for d in dst:
            for k, v in acc.items():
                if d.r.get(k, 0) < v:
                    d.r[k] = v

    def final_wait_all(self, eng="sp"):
        for key, val in self.dma_cnt.items():
            self._wait(eng, (key, val))

    def emit(self):
        nc = self.nc
        keys = [e for e in ENGS if self.count[e] > 0] + sorted(self.dma_cnt.keys())
        with contextlib.ExitStack() as st:
            sems = {k: st.enter_context(nc.semaphore("s_" + k)) for k in keys}
            block = st.enter_context(nc.Block())

            def run(eng_name):
                def body(e):
                    for item in self.streams[eng_name]:
                        if item[0] == "wait":
                            e.wait_ge(sems[item[1]], item[2])
                        elif item[0] == "op":
                            item[1](e).then_inc(sems[eng_name], 1)
                        else:
                            _, out, in_, key = item
                            e.dma_start(out=out, in_=in_).then_inc(sems[key], 16)
                return body

            block.sync(run("sp"))
            block.tensor(run("pe"))
            block.scalar(run("act"))
            block.vector(run("dve"))
            block.gpsimd(run("pool"))


class RR:
    def __init__(self, items):
        self.items = items
        self.i = 0

    def get(self):
        it = self.items[self.i]
        self.i = (self.i + 1) % len(self.items)
        return it


def build(n_seq=2, n_layers=2, dbg_stop=None, dbg_y=False):
    nc = bass.Bass("TRN2", target_bir_lowering=False)
    P = Prog(nc)

    def din(name, shape):
        return nc.dram_tensor(name, shape, F32, kind="ExternalInput").ap()

    x_h = din("x", [n_seq, S, D])
    wp_h = din("wp", [DEPTH, 8, 128, NKC * 512])
    wf_h = din("wf", [DEPTH, 128, NKC * 6])
    wo_h = din("wo", [DEPTH, 2, 128, NKC * 512])
    bf_h = din("bfg", [DEPTH, 6, 1])
    gn_h = din("gng", [DEPTH, 1, 384])
    lg_h = din("lng", [DEPTH, 1, D])
    lb_h = din("lnb", [DEPTH, 1, D])
    c128_h = din("c128", [128, 8 * 128])
    oh_h = din("oh", [128, 32])
    msel_h = din("msel", [16, S])
    perm_h = din("perm", [128, 128])
    cos_h = din("cost", [128, S])
    sin_h = din("sint", [128, S])
    iota_h = din("iota", [128, 512])
    dec_h = din("dec", [128, 3 * 10])
    out_h = nc.dram_tensor("out", [n_seq, S, D], F32, kind="ExternalOutput").ap()
    if dbg_y:
        dbgy_h = nc.dram_tensor("dbgy", [128, NKC * S], BF16, kind="ExternalOutput").ap()

    with contextlib.ExitStack() as st:
        def sb(name, shape, dt=BF16):
            return st.enter_context(nc.sbuf_tensor(name, shape, dt))

        def psum(name):
            return st.enter_context(nc.psum_tensor(name, [128, 512], F32))

        xres = sb("xres", [128, NB, D], F32)
        Rxres = [Res() for _ in range(NB)]
        xT = sb("xT", [128, NKC, S])
        RxT = [Res() for _ in range(NT)]
        yT = sb("yT", [128, NKC, S])
        RyT = [[Res() for _ in range(NT)] for _ in range(8)]
        wbuf = [sb("wbuf%d" % i, [128, NKC, 512]) for i in range(2)]
        Rw = [Res() for _ in range(2)]
        wfb2 = sb("wfb", [128, DEPTH, NKC, 6]); Rwf = Res()
        QA = sb("QA", [128, S]); QB = sb("QB", [128, S]); KA = sb("KA", [128, S]); KB = sb("KB", [128, S])
        RQ = [[Res() for _ in range(NT)] for _ in range(2)]
        RK = [[Res() for _ in range(NT)] for _ in range(2)]
        RQaug = [Res(), Res()]
        RKaug = [Res(), Res()]
        Qh = [QA, QB]; Kh = [KA, KB]
        VA = sb("VA", [128, NB, 128]); VB = sb("VB", [128, NB, 128])
        Vh = [VA, VB]
        RV = [[Res() for _ in range(NT)] for _ in range(2)]
        RVones = Res()
        sgA = sb("sgA", [128, S]); sgB = sb("sgB", [128, S])
        sgh = [sgA, sgB]
        Rsg = [[Res() for _ in range(NT)] for _ in range(2)]
        SPall = sb("SPall", [128, NB * 512])
        SPb = [SPall[:, i * 512:(i + 1) * 512] for i in range(NB)]
        RSP = [Res() for _ in range(NB)]
        Ktok = SPall[:, 0:2048].rearrange("p (b c) -> p b c", b=NB); RKtok = [Res() for _ in range(NT)]
        Pt = RR([(sb("Pt%d" % i, [128, 512]), Res()) for i in range(2)])
        f32t = RR([(sb("f32t%d" % i, [128, 512], F32), Res()) for i in range(3)])
        Ebuf = f32t
        qdpool = RR(f32t.items[0:2])
        rot = [SPall[:, 2048:3072].bitcast(F32), SPall[:, 3072:4096].bitcast(F32)]
        Rrot = Res()
        xb = RR([(sgA[:, 0:1024], Res()), (sgA[:, 1024:2048], Res())])
        xb_res = [xb.items[0][1], xb.items[1][1]]
        csb = RR([(sb("csb%d" % i, [16, 512]), Res()) for i in range(1)])
        negcspT = SPall[0:6, 0:2048]; Rncsp = [Res() for _ in range(NT)]
        csp_tok = sb("csp_tok", [128, NB, 6], F32); Rcsptok = [Res() for _ in range(NT)]
        cs_tmp = RR([(SPall[0:6, 4096 + i * 1024:5120 + i * 1024].bitcast(F32), Res()) for i in range(2)])
        ft1 = RR([(SPall[0:6, 6144 + i * 1024:7168 + i * 1024].bitcast(F32), Res()) for i in range(2)])
        ones6 = SPall[0:6, 2048:3072].bitcast(F32); Rones6 = Res()
        negb = sb("negb", [6, 1], F32); Rnegb = Res()
        braw = sb("braw", [6, 1], F32); Rbraw = Res()
        gng = sb("gng_sb", [128, 384], F32); Rgng = Res()
        lngain = KA[:].bitcast(F32); Rlng = Res()
        lnbias = KB[:].bitcast(F32); Rlnb = Res()
        c128 = sb("c128_sb", [128, 8 * 128]); Rc = Res()
        identf = sb("identf", [6, 8], F32)
        oh = sb("oh_sb", [128, 32])
        iota = SPall[:, 4096:5120].bitcast(F32); Riota = Res()
        dec = sb("dec_sb", [128, 30], F32)
        mask2 = sb("mask2", [128, 256])
        permf = sb("permf", [128, 128], F32)
        Sall = SPall[:, 5120:7168].rearrange("p (b c) -> p b c", b=NB)
        RSall = [Res() for _ in range(NB)]
        st8 = RR([(sb("st8_%d" % i, [128, 8, 6], F32), Res()) for i in range(2)])
        mv8 = RR([(sb("mv8_%d" % i, [128, 8, 2], F32), Res()) for i in range(2)])
        rs8 = RR([(sb("rs8_%d" % i, [128, 8], F32), Res()) for i in range(2)])
        lnst = RR([(sb("lnst%d" % i, [128, 2, 6], F32), Res()) for i in range(4)])
        lnmv = RR([(sb("lnmv%d" % i, [128, 4], F32), Res()) for i in range(4)])
        zt = RR([(QA[:].bitcast(F32), Res()), (QB[:].bitcast(F32), Res()),
                 (VA[:].rearrange("p b c -> p (b c)").bitcast(F32), Res()), (VB[:].rearrange("p b c -> p (b c)").bitcast(F32), Res())])
        qk_res = [r for grp in (RQ, RK, RV) for hh in grp for r in hh] + RQaug + RKaug + [RVones]
        ln_res = [it[1] for it in zt.items] + [Rlng, Rlnb]

        ident = c128[:, 0:128]
        m01 = c128[:, 128:256]
        m01s = c128[:, 256:384]
        negfox = c128[:, 384:512]
        negsb = c128[:, 512:640]
        neguinc = c128[:, 640:768]
        negones = c128[:, 768:896]
        zeros128 = c128[:, 896:1024]

        pbig = RR([(psum("pb%d" % i), Res()) for i in range(3)])
        pobank = RR([(psum("po%d" % i), Res()) for i in range(2)])
        pproj = RR([(psum("pp%d" % i), Res()) for i in range(2)])
        pmisc = RR([(psum("pm%d" % i), Res()) for i in range(1)])
        pln = RR(pproj.items + pbig.items + pobank.items)
        sbbig = RR(pbig.items + pmisc.items)
        ppT = RR(list(pproj.items))
        if not INTERLEAVE:
            pproj = RR(pproj.items + pbig.items)
            pvproj = RR(pmisc.items + pobank.items)
        else:
            pvproj = pmisc

        print("sbuf bytes remaining:", nc.sbuf_bytes_remaining)

        P.dma("pool", c128[:], c128_h, writes=[Rc])
        P.dma("pool", oh[:], oh_h, writes=[Rc])
        P.dma("sp", identf[:], c128_h[0:6, 0:8], writes=[Rc])
        P.dma("sp", dec[:], dec_h, writes=[Rc])
        P.dma("sp", permf[:], perm_h, writes=[Rc])
        P.dma("pool", mask2[:, 0:128], c128_h[:, 128:256], writes=[Rc])
        P.dma("pool", mask2[:, 128:256], c128_h[:, 128:256], writes=[Rc])
        fox_scr = Rncsp + [Rones6] + [it[1] for it in cs_tmp.items] + [it[1] for it in ft1.items]
        ret_scr = RKtok + [Rrot, Riota] + RSall
        sb_scr = RSP

        def ones_init(e):
            e.memset(VA[:, :, 64:128], 1.0)
            return e.memset(VB[:, :, 64:128], 1.0)
        P.op("pool", ones_init, writes=[RVones])
        for l_ in range(DEPTH):
            P.dma("pool", wfb2[:, l_, :, :].rearrange("p k c -> p (k c)"), wf_h[l_], writes=[Rwf])

        wstate = {"i": 0}

        def load_w(src_ap):
            i = wstate["i"]
            wstate["i"] = 1 - i
            for kc in range(NKC):
                P.dma("pool", wbuf[i][:, kc, :], src_ap[:, kc * 512:(kc + 1) * 512], writes=[Rw[i]])
            return wbuf[i], Rw[i]

        def make_xT(tb, evac_eng="dve"):
            P.handoff(Rsg[0] + Rsg[1], xb_res)
            xbt, Rxb = xb.get()
            P.op("act", lambda e: e.activation(out=xbt[:], in_=xres[:, tb, :], func=AF.Copy), reads=[Rxres[tb]], writes=[Rxb])
            pm, Rpm = pmisc.get()
            pmb = pm[:].bitcast(BF16)

            def tr(e):
                ins = None
                for kc in range(NKC):
                    ins = e.transpose(out=pmb[:, kc * 128:(kc + 1) * 128], in_=xbt[:, kc * 128:(kc + 1) * 128], identity=ident)
                return ins
            P.op("pe", tr, reads=[Rxb, Rc], writes=[Rpm])
            if evac_eng == "dve":
                P.op("dve", lambda e: e.tensor_copy(out=xT[:, :, tb * 128:(tb + 1) * 128],
                                                    in_=pmb.rearrange("p (k t) -> p k t", k=NKC)),
                     reads=[Rpm], writes=[RxT[tb // 4]])
            else:
                P.op("act", lambda e: e.activation(out=xT[:, :, tb * 128:(tb + 1) * 128],
                                                   in_=pmb.rearrange("p (k t) -> p k t", k=NKC), func=AF.Copy),
                     reads=[Rpm], writes=[RxT[tb // 4]])
            P.handoff(xb_res, Rsg[0] + Rsg[1])

        def make_xT_stages(tb):
            G = {}

            def cast():
                P.handoff(Rsg[0] + Rsg[1], xb_res)
                xbt, Rxb = xb.get()
                G["xb"] = (xbt, Rxb)
                P.op("act", lambda e: e.activation(out=xbt[:], in_=xres[:, tb, :], func=AF.Copy), reads=[Rxres[tb]], writes=[Rxb])

            def trans():
                xbt, Rxb = G["xb"]
                pm, Rpm = pmisc.get()
                pmb = pm[:].bitcast(BF16)
                G["pm"] = (pmb, Rpm)

                def tr(e):
                    ins = None
                    for kc in range(NKC):
                        ins = e.transpose(out=pmb[:, kc * 128:(kc + 1) * 128], in_=xbt[:, kc * 128:(kc + 1) * 128], identity=ident)
                    return ins
                P.op("pe", tr, reads=[Rxb, Rc], writes=[Rpm])

            def evac():
                pmb, Rpm = G["pm"]
                P.op("act", lambda e: e.activation(out=xT[:, :, tb * 128:(tb + 1) * 128],
                                                   in_=pmb.rearrange("p (k t) -> p k t", k=NKC), func=AF.Copy),
                     reads=[Rpm], writes=[RxT[tb // 4]])
                P.handoff(xb_res, Rsg[0] + Rsg[1])
            return cast, trans, evac

        def proj_fm(wb, Rwb, c0, t):
            pb, Rpb = pproj.get()

            def mm(e):
                ins = None
                for kc in range(NKC):
                    ins = e.matmul(pb[:, :], lhsT=wb[:, kc, c0:c0 + 128], rhs=xT[:, kc, t * 512:(t + 1) * 512],
                                   start=(kc == 0), stop=(kc == NKC - 1))
                return ins
            P.op("pe", mm, reads=[Rwb, RxT[t]], writes=[Rpb])
            return pb, Rpb

        def proj_v(wb, Rwb, t, eng="dve"):
            pb, Rpb = proj_fm(wb, Rwb, 256, t)
            vt, Rvt = Pt.get()
            if eng == "dve":
                P.op("dve", lambda e: e.tensor_copy(out=vt[:], in_=pb[:, :]), reads=[Rpb], writes=[Rvt])
            else:
                P.op("act", lambda e: e.activation(out=vt[:], in_=pb[:, :], func=AF.Copy), reads=[Rpb], writes=[Rvt])
            pm, Rpm = pvproj.get()
            pmb = pm[:].bitcast(BF16)

            def tr(e):
                ins = None
                for j in range(4):
                    ins = e.transpose(out=pmb[:, j * 128:(j + 1) * 128], in_=vt[:, j * 128:(j + 1) * 128], identity=ident)
                return ins
            P.op("pe", tr, reads=[Rvt, Rc], writes=[Rpm])
            pv = pmb[:, 0:512].rearrange("p (j c) -> p j c", j=4)
            if eng == "dve":
                P.op("dve", lambda e: e.tensor_copy(out=VA[:, 4 * t:4 * t + 4, 0:64], in_=pv[:, :, 0:64]),
                     reads=[Rpm, RVones], writes=[RV[0][t]])
                P.op("dve", lambda e: e.tensor_copy(out=VB[:, 4 * t:4 * t + 4, 0:64], in_=pv[:, :, 64:128]),
                     reads=[Rpm, RVones], writes=[RV[1][t]])
            else:
                P.op("act", lambda e: e.activation(out=VA[:, 4 * t:4 * t + 4, 0:64], in_=pv[:, :, 0:64], func=AF.Copy),
                     reads=[Rpm, RVones], writes=[RV[0][t]])
                P.op("act", lambda e: e.activation(out=VB[:, 4 * t:4 * t + 4, 0:64], in_=pv[:, :, 64:128], func=AF.Copy),
                     reads=[Rpm, RVones], writes=[RV[1][t]])

        def fox_prep(layer):
            P.handoff(sb_scr + ret_scr, fox_scr)
            P.op("pool", lambda e: e.memset(ones6[:], 1.0), writes=[Rones6])
            wfb = wfb2[:, layer, :, :]
            P.dma("sp", braw[:], bf_h[layer], writes=[Rbraw])
            P.op("dve", lambda e: e.tensor_scalar(out=negb[:], in0=braw[:], scalar1=-1.0, scalar2=None, op0=ALU.mult),
                 reads=[Rbraw], writes=[Rnegb])
            def kaug(e):
                e.memset(KA[64:65, :], 1.0)
                return e.memset(KB[64:65, :], 1.0)
            P.op("pool", kaug, writes=[RKaug[0], RKaug[1]] + RK[1])
            prev = None
            for t in range(NT):
                pm, Rpm = pmisc.get()

                def mm(e, pm=pm, t=t):
                    ins = None
                    for kc in range(NKC):
                        ins = e.matmul(pm[0:6, :], lhsT=wfb[:, kc, :], rhs=xT[:, kc, t * 512:(t + 1) * 512],
                                       start=(kc == 0), stop=(kc == NKC - 1))
                    return ins
                P.op("pe", mm, reads=[Rwf, RxT[t]], writes=[Rpm])
                f1, Rf1 = ft1.get()
                P.op("act", lambda e, pm=pm, f1=f1: e.activation(out=f1[:], in_=pm[0:6, :], func=AF.Exp, bias=negb[:], scale=-1.0),
                     reads=[Rpm, Rnegb], writes=[Rf1])
                P.op("act", lambda e, f1=f1: e.activation(out=f1[:], in_=f1[:], func=AF.Ln, bias=1.0, scale=1.0),
                     reads=[Rf1], writes=[Rf1])
                cs, Rcs = cs_tmp.get()
                if prev is None:
                    init = 0.0
                    rd = [Rf1, Rones6]
                else:
                    init = prev[0][:, 511:512]
                    rd = [Rf1, Rones6, prev[1]]
                P.op("dve", lambda e, cs=cs, f1=f1, init=init: e.tensor_tensor_scan(
                    out=cs[:], data0=ones6[:], data1=f1[:], initial=init, op0=ALU.mult, op1=ALU.add),
                    reads=rd, writes=[Rcs])
                prev = (cs, Rcs)
                P.op("dve", lambda e, cs=cs, t=t: e.tensor_scalar(out=negcspT[:, t * 512:(t + 1) * 512], in0=cs[:], scalar1=-1.0,
                                                                  scalar2=None, op0=ALU.mult),
                     reads=[Rcs], writes=[Rncsp[t]])
                pm2, Rpm2 = pmisc.get()

                def tr(e, pm2=pm2, cs=cs):
                    ins = None
                    for j in range(4):
                        ins = e.transpose(out=pm2[:, j * 6:(j + 1) * 6], in_=cs[:, j * 128:(j + 1) * 128], identity=identf[0:6, 0:6])
                    return ins
                P.op("pe", tr, reads=[Rcs, Rc], writes=[Rpm2])
                P.op("dve", lambda e, pm2=pm2, t=t: e.tensor_copy(out=csp_tok[:, 4 * t:4 * t + 4, :],
                                                                  in_=pm2[:, 0:24].rearrange("p (j c) -> p j c", j=4)),
                     reads=[Rpm2], writes=[Rcsptok[t]])

        def inproj_headwise(wb, Rwb, kind):
            def setup():
                if kind == "sb":
                    P.handoff(ret_scr + fox_scr, sb_scr)
                    P.dma("pool", KA[64:80, :], msel_h, writes=[RKaug[0]] + RK[1])
                    P.dma("pool", KB[64:80, :], msel_h, writes=[RKaug[1]])

            def q_item(t):
                ts = slice(t * 512, (t + 1) * 512)
                pb, Rpb = proj_fm(wb, Rwb, 0, t)
                P.op("act", lambda e: e.activation(out=QA[0:64, ts], in_=pb[0:64, :], func=AF.Copy, scale=0.125),
                     reads=[Rpb], writes=[RQ[0][t]])
                P.op("dve", lambda e: e.tensor_scalar(out=QB[0:64, ts], in0=pb[64:128, :], scalar1=0.125, scalar2=None, op0=ALU.mult),
                     reads=[Rpb], writes=[RQ[1][t]])

            def k_item(t):
                ts = slice(t * 512, (t + 1) * 512)
                pb, Rpb = proj_fm(wb, Rwb, 128, t)
                P.op("act", lambda e: e.activation(out=KA[0:64, ts], in_=pb[0:64, :], func=AF.Copy), reads=[Rpb], writes=[RK[0][t]])
                P.op("dve", lambda e: e.tensor_copy(out=KB[0:64, ts], in_=pb[64:128, :]), reads=[Rpb], writes=[RK[1][t]])

            def g_item(t):
                ts = slice(t * 512, (t + 1) * 512)
                pb, Rpb = proj_fm(wb, Rwb, 384, t)
                P.op("act", lambda e: e.activation(out=sgA[0:64, ts], in_=pb[0:64, :], func=AF.Silu), reads=[Rpb], writes=[Rsg[0][t]])
                P.op("act", lambda e: e.activation(out=sgB[0:64, ts], in_=pb[64:128, :], func=AF.Silu), reads=[Rpb], writes=[Rsg[1][t]])

            qkv = [[(lambda t=t: q_item(t)), (lambda t=t: k_item(t)), (lambda t=t: proj_v(wb, Rwb, t))] for t in range(NT)]
            gate = [(lambda t=t: g_item(t)) for t in range(NT)]
            return setup, qkv, gate

        def inproj_ret(wb, Rwb, rp):
            def setup():
                P.handoff(sb_scr + fox_scr, ret_scr)
                if rp == 0:
                    P.dma("sp", iota[:], iota_h, writes=[Riota])

            def qk_stages(t, which):
                G = {}

                def A():
                    qk_item(t, which, G, "A")

                def B():
                    qk_item(t, which, G, "B")
                return (A, B)

            def qk_item(t, which, G=None, part="AB"):
                ts = slice(t * 512, (t + 1) * 512)
                c0, dst, Rdst, Raug = ((0, QA, RQ, RQaug[0]), (128, KA, RK, RKaug[0]))[which]
                def load_rot():
                    P.dma("sp", rot[0][:], cos_h[:, ts], writes=[Rrot])
                    P.dma("sp", rot[1][:], sin_h[:, ts], writes=[Rrot])
                if part == "B":
                    if which == 0:
                        load_rot()
                    qd, Rqd = G["qd"]
                    return qk_tail(t, which, qd, Rqd, dst, Rdst, Raug, ts)
                if which == 0 and part == "AB":
                    load_rot()
                pb, Rpb = proj_fm(wb, Rwb, c0, t)
                dt_, Rd = f32t.items[2]
                col = rp * 10 + which
                bcol = rp * 10 + 2 + which * 4 + t
                P.op("act", lambda e: e.activation(out=dt_[:], in_=iota[:], func=AF.Exp, bias=dec[:, bcol:bcol + 1],
                                                   scale=dec[:, col:col + 1]),
                     reads=[Rc, Riota], writes=[Rd])
                qd, Rqd = qdpool.get()
                P.op("dve", lambda e: e.tensor_tensor(out=qd[:], in0=pb[:, :], in1=dt_[:], op=ALU.mult), reads=[Rpb, Rd], writes=[Rqd])
                if part == "A":
                    G["qd"] = (qd, Rqd)
                    return
                qk_tail(t, which, qd, Rqd, dst, Rdst, Raug, ts)

            def qk_tail(t, which, qd, Rqd, dst, Rdst, Raug, ts):
                u, Ru = f32t.items[2]
                psw, Rpsw = ppT.get()
                P.op("pe", lambda e: e.matmul(psw[:, :], lhsT=permf[:], rhs=qd[:], start=True, stop=True),
                     reads=[Rqd, Rc], writes=[Rpsw])
                P.op("dve", lambda e: e.tensor_tensor(out=u[:], in0=psw[:, :], in1=rot[1][:], op=ALU.mult),
                     reads=[Rpsw, Rrot], writes=[Ru])
                P.op("dve", lambda e: e.tensor_tensor(out=qd[:], in0=qd[:], in1=rot[0][:], op=ALU.mult),
                     reads=[Rqd, Rrot, Ru], writes=[Rqd])
                P.op("dve", lambda e: e.tensor_tensor(out=dst[:, ts], in0=qd[:], in1=u[:], op=ALU.add),
                     reads=[Rqd, Ru], writes=[Rdst[0][t], Rdst[1][t], Raug])

            def g_item(t):
                ts = slice(t * 512, (t + 1) * 512)
                pb, Rpb = proj_fm(wb, Rwb, 384, t)
                P.op("act", lambda e: e.activation(out=sgA[:, ts], in_=pb[:, :], func=AF.Silu), reads=[Rpb], writes=[Rsg[0][t], Rsg[1][t]])

            qkv = [[(lambda t=t: qk_item(t, 0)), (lambda t=t: qk_item(t, 1)), (lambda t=t: proj_v(wb, Rwb, t, "act"))] for t in range(NT)]
            gate = [(lambda t=t: g_item(t)) for t in range(NT)]
            stages = []
            for t in range(NT):
                stages += [qk_stages(t, 0), qk_stages(t, 1)]
            piped = []
            for i in range(len(stages) + 1):
                if i < len(stages):
                    piped.append(stages[i][0])
                if i >= 1:
                    piped.append(stages[i - 1][1])
                    if (i - 1) % 2 == 1:
                        piped.append(lambda t=(i - 1) // 2: proj_v(wb, Rwb, t, "act"))
            qkv = [piped, [], [], []]
            return setup, qkv, gate

        def flatten(items, depth):
            out = []
            n = len(items)
            for i in range(n + depth):
                if i < n:
                    out.append(items[i][0])
                if i >= depth:
                    out.append(items[i - depth][1])
            return out

        def run_merged(main, fill):
            n, m = len(main), len(fill)
            if m == 0:
                for f in main:
                    f()
                return
            stride = max(1, n // (m + 1))
            k = 0
            for i, f in enumerate(main):
                f()
                if k < m and (i + 1) % stride == 0:
                    fill[k]()
                    k += 1
            while k < m:
                fill[k]()
                k += 1

        def pipeline(items, depth):
            n = len(items)
            for i in range(n + depth):
                if i < n:
                    items[i][0]()
                if i >= depth:
                    items[i - depth][1]()

        def fox_items(hi, gh, hp, t):
            Q, K, V, sg = Qh[hi], Kh[hi], Vh[hi], sgh[hi]
            r0 = hi * 64
            items = []
            if True:
                nkb = 4 * (t + 1)
                tstate = {}
                for kb in range(nkb):
                    j = kb - 4 * t
                    c0 = max(j, 0) * 128
                    ks = slice(kb * 128, (kb + 1) * 128)
                    st_ = {}

                    def A(st_=st_, ks=ks, c0=c0, j=j, t=t, kb=kb):
                        ps, Rps = pbig.get()
                        st_["ps"] = (ps, Rps)

                        def mm(e):
                            ins = e.matmul(ps[:, c0:512], lhsT=K[0:65, ks], rhs=Q[0:65, t * 512 + c0:(t + 1) * 512],
                                           start=True, stop=(j < 0))
                            if j >= 0:
                                ins = e.matmul(ps[:, c0:c0 + 128], lhsT=ident, rhs=negfox, start=False, stop=True,
                                               skip_group_check=True)
                            return ins
                        P.op("pe", mm, reads=[RK[hi][kb // 4], RKaug[hi], RQ[hi][t], RQaug[hi], Rc], writes=[Rps])

                    def B(st_=st_, tstate=tstate, c0=c0, kb=kb, nkb=nkb, t=t):
                        ps, Rps = st_["ps"]
                        if kb == 0:
                            tstate["po"] = pobank.get()
                        po, Rpo = tstate["po"]
                        pt, Rpt = Pt.get()
                        P.op("act", lambda e: e.activation(out=pt[:, c0:512], in_=ps[:, c0:512], func=AF.Exp,
                                                           bias=csp_tok[:, kb, gh:gh + 1], scale=1.0),
                             reads=[Rps, Rcsptok[kb // 4]], writes=[Rpt])
                        P.op("pe", lambda e: e.matmul(po[:, c0:512], lhsT=V[:, kb, :], rhs=pt[:, c0:512], start=(kb == 0),
                                                      stop=(kb == nkb - 1), skip_group_check=True),
                             reads=[Rpt, RV[hi][kb // 4], RVones], writes=[Rpo])
                        if kb == nkb - 1:
                            ts = slice(t * 512, (t + 1) * 512)
                            rc, Rrc = f32t.get()
                            P.op("dve", lambda e: e.reciprocal(out=rc[0:64, :], in_=po[64:128, :]), reads=[Rpo], writes=[Rrc])
                            P.op("dve", lambda e: e.tensor_tensor(out=rc[0:64, :], in0=rc[0:64, :], in1=sg[0:64, ts], op=ALU.mult),
                                 reads=[Rrc, Rsg[hi][t]], writes=[Rrc])
                            P.op("dve", lambda e: e.tensor_tensor(out=yT[r0:r0 + 64, hp, ts], in0=po[0:64, :], in1=rc[0:64, :],
                                                                  op=ALU.mult),
                                 reads=[Rpo, Rrc], writes=[RyT[hp][t]])
                    items.append((A, B))
            return items

        def sb_stream(hi, hp, t, raw=False):
            Q, K, V, sg = Qh[hi], Kh[hi], Vh[hi], sgh[hi]
            r0 = hi * 64
            stream = []
            pcs_box = {}
            if True:
                nkb = 4 * (t + 1)
                items = []
                for kb in range(nkb):
                    j = kb - 4 * t
                    c0 = max(j, 0) * 128
                    r = nkb - 1 - kb
                    ks = slice(kb * 128, (kb + 1) * 128)
                    st_ = {}

                    def A(st_=st_, ks=ks, c0=c0, t=t, kb=kb):
                        pz, Rpz = sbbig.get()
                        st_["pz"] = (pz, Rpz)
                        P.op("pe", lambda e: e.matmul(pz[:, c0:512], lhsT=K[0:64, ks], rhs=Q[0:64, t * 512 + c0:(t + 1) * 512],
                                                      start=True, stop=True),
                             reads=[RK[hi][kb // 4], RQ[hi][t]], writes=[Rpz])

                    def B(st_=st_, c0=c0, kb=kb, j=j, r=r, nkb=nkb):
                        pz, Rpz = st_["pz"]
                        if kb == 0:
                            pcs_box["p"] = pobank.get()
                        pcs, Rpcs = pcs_box["p"]
                        eb, Reb = Ebuf.get()
                        P.op("act", lambda e: e.activation(out=eb[:, c0:512], in_=pz[:, c0:512], func=AF.Exp),
                             reads=[Rpz], writes=[Reb])
                        P.op("act", lambda e: e.activation(out=SPb[kb][:, c0:512], in_=eb[:, c0:512], func=AF.Ln, bias=1.0, scale=1.0),
                             reads=[Reb], writes=[RSP[kb]])
                        if j >= 0:
                            P.op("dve", lambda e: e.tensor_tensor(out=SPb[kb][:, c0:c0 + 128], in0=SPb[kb][:, c0:c0 + 128],
                                                                   in1=m01s, op=ALU.mult),
                                 reads=[RSP[kb], Rc], writes=[RSP[kb]])
                        P.op("pe", lambda e: e.matmul(pcs[0:16, c0:512], lhsT=oh[:, 15 - kb:31 - kb], rhs=SPb[kb][:, c0:512],
                                                      start=(kb == 0), stop=(kb == nkb - 1), skip_group_check=True),
                             reads=[RSP[kb], Rc], writes=[Rpcs])
                    items.append((A, B))
                p1_items = items
                stream += flatten(items, 2)

                def cscopy():
                    pcs, Rpcs = pcs_box["p"]
                    P.op("dve", lambda e: e.tensor_copy(out=Q[64:80, t * 512:(t + 1) * 512], in_=pcs[0:16, :]),
                         reads=[Rpcs], writes=[RQaug[hi]])
                stream.append(cscopy)
                tstate = {}
                items = []
                for kb in range(nkb):
                    j = kb - 4 * t
                    c0 = max(j, 0) * 128
                    r = nkb - 1 - kb
                    ks = slice(kb * 128, (kb + 1) * 128)
                    st_ = {}

                    def A2(st_=st_, ks=ks, c0=c0, j=j, r=r, t=t, kb=kb):
                        pa, Rpa = sbbig.get()
                        st_["pa"] = (pa, Rpa)

                        def mm(e):
                            e.matmul(pa[:, c0:512], lhsT=K[0:80, ks], rhs=Q[0:80, t * 512 + c0:(t + 1) * 512], start=True, stop=False)
                            ins = e.matmul(pa[:, c0:512], lhsT=neguinc, rhs=SPb[kb][:, c0:512], start=False, stop=(j < 0),
                                           skip_group_check=True)
                            if j >= 0:
                                ins = e.matmul(pa[:, c0:c0 + 128], lhsT=ident, rhs=negsb, start=False, stop=True, skip_group_check=True)
                            return ins
                        P.op("pe", mm, reads=[RK[hi][kb // 4], RKaug[hi], RQ[hi][t], RQaug[hi], RSP[kb], Rc], writes=[Rpa])

                    def B2(st_=st_, tstate=tstate, c0=c0, kb=kb, nkb=nkb, t=t):
                        pa, Rpa = st_["pa"]
                        if kb == 0:
                            tstate["po"] = pobank.get()
                        po, Rpo = tstate["po"]
                        pt, Rpt = Pt.get()
                        P.op("act", lambda e: e.activation(out=pt[:, c0:512], in_=pa[:, c0:512], func=AF.Exp),
                             reads=[Rpa], writes=[Rpt])
                        P.op("pe", lambda e: e.matmul(po[:, c0:512], lhsT=V[:, kb, :], rhs=pt[:, c0:512], start=(kb == 0),
                                                      stop=(kb == nkb - 1), skip_group_check=True),
                             reads=[Rpt, RV[hi][kb // 4], RVones], writes=[Rpo])
                        if kb == nkb - 1:
                            ts = slice(t * 512, (t + 1) * 512)
                            P.op("dve", lambda e: e.tensor_tensor(out=yT[r0:r0 + 64, hp, ts], in0=po[0:64, :], in1=sg[0:64, ts],
                                                                  op=ALU.mult),
                                 reads=[Rpo, Rsg[hi][t]], writes=[RyT[hp][t]])
                    items.append((A2, B2))
                stream += flatten(items, 2)
            if raw:
                return p1_items, cscopy, items
            return stream

        def sb_pair(hp):
            units = [sb_stream(hi, hp, t, raw=True) for t in range(NT) for hi in range(2)]
            pipeline(units[0][0], 2)
            units[0][1]()
            for i in range(len(units)):
                p2 = units[i][2]
                p1n = units[i + 1][0] if i + 1 < len(units) else []
                n = max(len(p2), len(p1n))
                for k in range(n + 1):
                    if k < len(p2):
                        p2[k][0]()
                    if k < len(p1n):
                        p1n[k][0]()
                    if SB_DUMMY:
                        jb, Rjb = pproj.items[0]

                        def dmm(e, jb=jb):
                            ins = None
                            for _ in range(SB_DUMMY):
                                ins = e.matmul(jb[:, :], lhsT=ident, rhs=c128[:, 0:512], start=True, stop=True)
                            return ins
                        P.op("pe", dmm, reads=[Rc], writes=[Rjb])
                    if 0 <= k - 1 < len(p2):
                        p2[k - 1][1]()
                    if 0 <= k - 1 < len(p1n):
                        p1n[k - 1][1]()
                if i + 1 < len(units):
                    units[i + 1][1]()

        def ret_stream(rp, hp, tg, gstate, raw=False):
            def ktok():
                pm, Rpm = ppT.get()
                pmb = pm[:].bitcast(BF16)

                def tr(e):
                    ins = None
                    for jj in range(4):
                        c = 4 * tg + jj
                        ins = e.transpose(out=pmb[:, jj * 128:(jj + 1) * 128], in_=KA[:, c * 128:(c + 1) * 128], identity=ident)
                    return ins
                P.op("pe", tr, reads=[RK[0][tg], RK[1][tg], Rc], writes=[Rpm])
                P.op("act", lambda e: e.activation(out=Ktok[:, 4 * tg:4 * tg + 4, :],
                                                   in_=pmb[:, 0:512].rearrange("p (j c) -> p j c", j=4), func=AF.Copy),
                     reads=[Rpm], writes=[RKtok[tg]])
            items = [ret_chunk_item(rp, hp, c, gstate) for c in range(4 * tg, 4 * tg + 4)]
            if raw:
                return ktok, items
            return [ktok] + flatten(items, 1)

        def ret_chunk_item(rp, hp, c, gstate):
            cg, ci = c // 4, c % 4
            cs_ = slice(c * 128, (c + 1) * 128)
            st_ = {}

            def A():
                pst, Rpst = pbig.get()
                st_["p"] = (pst, Rpst, pst, Rpst)

                pS, RpS = pmisc.items[0]

                def mm(e):
                    e.matmul(pst[:, 0:128], lhsT=KA[0:64, cs_], rhs=QA[0:64, cs_], start=True, stop=True)
                    if c == 0:
                        e.matmul(pS[:, 0:128], lhsT=zeros128, rhs=c128[:, 0:128], start=True, stop=False, skip_group_check=True)
                    e.matmul(pS[:, 0:64], lhsT=Ktok[:, c, :], rhs=VA[:, c, 0:64], start=False, stop=(c == NB - 1),
                             skip_group_check=True)
                    e.matmul(pS[:, 64:128], lhsT=Ktok[:, c, :], rhs=VB[:, c, 0:64], start=False, stop=(c == NB - 1),
                             skip_group_check=True)
                    return e.matmul(pst[:, 128:256], lhsT=KA[64:128, cs_], rhs=QA[64:128, cs_], start=True, stop=True)
                P.op("pe", mm, reads=[RK[0][cg], RK[1][cg], RQ[0][cg], RQ[1][cg], RKtok[cg], RV[0][cg], RV[1][cg], Rc],
                     writes=[Rpst, RpS])
                if c < NB - 1:
                    P.op("act", lambda e: e.activation(out=Sall[:, c + 1, :], in_=pS[:, 0:128], func=AF.Copy),
                         reads=[RpS], writes=[RSall[c + 1]])

            def B():
                pst, Rpst, pst2, Rpst2 = st_["p"]
                if ci == 0:
                    gstate["po"] = pobank.get()
                po, Rpo = gstate["po"]
                pt, Rpt = Pt.get()

                P.op("dve", lambda e: e.tensor_tensor(out=pt[:, 0:256], in0=pst[:, 0:256], in1=mask2[:], op=ALU.mult),
                     reads=[Rpst, Rc], writes=[Rpt])

                def om(e):
                    o0 = ci * 128
                    e.matmul(po[:, o0:o0 + 64], lhsT=pt[:, 0:128], rhs=VA[:, c, 0:64], start=True, stop=(c == 0))
                    if c > 0:
                        e.matmul(po[:, o0:o0 + 64], lhsT=QA[0:64, cs_], rhs=Sall[0:64, c, 0:64], start=False, stop=True)
                    ins = e.matmul(po[:, o0 + 64:o0 + 128], lhsT=pt[:, 128:256], rhs=VB[:, c, 0:64], start=True, stop=(c == 0))
                    if c > 0:
                        ins = e.matmul(po[:, o0 + 64:o0 + 128], lhsT=QA[64:128, cs_], rhs=Sall[64:128, c, 64:128], start=False, stop=True)
                    return ins
                P.op("pe", om, reads=[Rpt, RV[0][cg], RV[1][cg], RQ[0][cg], RQ[1][cg], RSall[c]], writes=[Rpo])
                pend = gstate.setdefault("pend", [])
                for ent in list(pend):
                    ent[0] -= 1
                    if ent[0] <= 0:
                        pend.remove(ent)
                        ent[1]()
                if ci == 3:
                    st1, st2, st2b, st3 = ret_gn(rp, hp, cg, po, Rpo)
                    st1()
                    pend.append([1, st2])
                    pend.append([2, st2b])
                    pend.append([3, st3])
            return (A, B)

        def ret_gn(rp, hp, cg, po, Rpo):
            G = {}

            def stage1():
                if True:
                    s8, Rs8 = st8.get()
                    pov = po[:].rearrange("p (g d) -> p g d", g=8)
                    def gstats(e, s8=s8, pov=pov):
                        ins = None
                        for g in range(8):
                            ins = e.bn_stats(out=s8[:, g, :], in_=pov[:, g, :])
                        return ins
                    P.op("dve", gstats, reads=[Rpo], writes=[Rs8])
                    m8, Rm8 = mv8.get()
                    def aggr(e, s8=s8, m8=m8):
                        ins = None
                        for g in range(8):
                            ins = e.bn_aggr(out=m8[:, g, :], in_=s8[:, g, :])
                        return ins
                    P.op("dve", aggr, reads=[Rs8], writes=[Rm8])
                    r8, Rr8 = rs8.get()
                    P.op("act", lambda e, r8=r8, m8=m8: e.activation(out=r8[:], in_=m8[:, :, 1], func=AF.Ln, bias=GN_EPS, scale=1.0),
                         reads=[Rm8], writes=[Rr8])
                    P.op("act", lambda e, r8=r8: e.activation(out=r8[:], in_=r8[:], func=AF.Exp, scale=-0.5),
                         reads=[Rr8], writes=[Rr8])
                    G.update(pov=pov, m8=m8, Rm8=Rm8, r8=r8, Rr8=Rr8)

            def stage2():
                if True:
                    pov, m8, Rm8, r8, Rr8 = G["pov"], G["m8"], G["Rm8"], G["r8"], G["Rr8"]
                    t1, Rt1 = f32t.get()
                    t1v = t1[:].rearrange("p (g d) -> p g d", g=8)

                    def gnorm(e, t1v=t1v, pov=pov, m8=m8, r8=r8):
                        ins = None
                        for g in range(8):
                            ins = e.tensor_scalar(out=t1v[:, g, :], in0=pov[:, g, :], scalar1=m8[:, g, 0:1], scalar2=r8[:, g:g + 1],
                                                  op0=ALU.subtract, op1=ALU.mult)
                        return ins
                    P.op("dve", gnorm, reads=[Rpo, Rm8, Rr8], writes=[Rt1])
                    pt2, Rpt2 = Pt.get()

                    def ggain(e, pt2=pt2, t1=t1):
                        ins = None
                        for cc in range(4):
                            ins = e.tensor_tensor(out=pt2[:, cc * 128:(cc + 1) * 128], in0=t1[:, cc * 128:(cc + 1) * 128],
                                                  in1=gng[:, rp * 128:(rp + 1) * 128], op=ALU.mult)
                        return ins
                    P.op("dve", ggain, reads=[Rt1, Rgng], writes=[Rpt2])
                    G.update(pt2=pt2, Rpt2=Rpt2)

            def stage2b():
                if True:
                    pt2, Rpt2 = G["pt2"], G["Rpt2"]
                    pm, Rpm = ppT.get()
                    pmb = pm[:].bitcast(BF16)

                    def tr(e, pmb=pmb, pt2=pt2):
                        ins = None
                        for jj in range(4):
                            ins = e.transpose(out=pmb[:, jj * 128:(jj + 1) * 128], in_=pt2[:, jj * 128:(jj + 1) * 128], identity=ident)
                        return ins
                    P.op("pe", tr, reads=[Rpt2, Rc], writes=[Rpm])
                    G.update(pmb=pmb, Rpm=Rpm)

            def stage3():
                if True:
                    pmb, Rpm = G["pmb"], G["Rpm"]
                    ts = slice(cg * 512, (cg + 1) * 512)
                    P.op("dve", lambda e, pmb=pmb, ts=ts: e.tensor_tensor(out=yT[:, hp, ts], in0=pmb[:, 0:512], in1=sgA[:, ts],
                                                                          op=ALU.mult),
                         reads=[Rpm, Rsg[0][cg], Rsg[1][cg]], writes=[RyT[hp][cg]])
            return [stage1, stage2, stage2b, stage3]

        def ln_item(s, tb, last, w0, Rw0, w1, Rw1):
            tks = slice(tb * 128, (tb + 1) * 128)
            st_ = {}

            def S0():
                pbs = []
                for half, (wb, Rwb) in enumerate(((w0, Rw0), (w1, Rw1))):
                    pb, Rpb = pln.get()
                    pbs.append((pb, Rpb))

                    def mm(e, pb=pb, wb=wb):
                        ins = None
                        for ec in range(NKC):
                            ins = e.matmul(pb[:, :], lhsT=yT[:, ec, tks], rhs=wb[:, ec, :], start=(ec == 0), stop=(ec == NKC - 1))
                        return ins
                    P.op("pe", mm, reads=[Rwb] + [RyT[c][tb // 4] for c in range(8)], writes=[Rpb])
                st_["pbs"] = pbs

            def S1():
                z, Rz = zt.get()
                for half, (pb, Rpb) in enumerate(st_["pbs"]):
                    hs = slice(half * 512, (half + 1) * 512)
                    P.op("dve", lambda e, pb=pb, hs=hs: e.scalar_tensor_tensor(
                        out=z[:, hs], in0=xres[:, tb, hs], scalar=ALPHA, in1=pb[:, :], op0=ALU.mult, op1=ALU.add),
                        reads=[Rpb, Rxres[tb]], writes=[Rz])
                ls, Rls = lnst.get()

                def lnstats(e):
                    e.bn_stats(out=ls[:, 0, :], in_=z[:, 0:512])
                    return e.bn_stats(out=ls[:, 1, :], in_=z[:, 512:1024])
                P.op("dve", lnstats, reads=[Rz], writes=[Rls])
                lm, Rlm = lnmv.get()
                P.op("dve", lambda e: e.bn_aggr(out=lm[:, 0:2], in_=ls[:].rearrange("p g s -> p (g s)")), reads=[Rls], writes=[Rlm])
                st_["z"] = (z, Rz, lm, Rlm)

            def S2():
                z, Rz, lm, Rlm = st_["z"]
                P.op("act", lambda e: e.activation(out=lm[:, 2:3], in_=lm[:, 1:2], func=AF.Ln, bias=LN_EPS, scale=1.0),
                     reads=[Rlm], writes=[Rlm])
                P.op("act", lambda e: e.activation(out=lm[:, 2:3], in_=lm[:, 2:3], func=AF.Exp, scale=-0.5),
                     reads=[Rlm], writes=[Rlm])

            def S3():
                z, Rz, lm, Rlm = st_["z"]
                P.op("dve", lambda e: e.scalar_tensor_tensor(out=z[:], in0=z[:], scalar=lm[:, 0:1], in1=lngain[:],
                                                             op0=ALU.subtract, op1=ALU.mult),
                     reads=[Rz, Rlm, Rlng], writes=[Rz])
                P.op("dve", lambda e: e.scalar_tensor_tensor(out=xres[:, tb, :], in0=z[:], scalar=lm[:, 2:3], in1=lnbias[:],
                                                             op0=ALU.mult, op1=ALU.add),
                     reads=[Rz, Rlm, Rlnb], writes=[Rxres[tb]])

            if last:
                def S4():
                    P.dma("sp", out_h[s, tks, :], xres[:, tb, :], reads=[Rxres[tb]])
                nop = lambda: None
                return [S0, S1, S2, S3, S4, nop, nop]
            cast, trans, evac = make_xT_stages(tb)
            return [S0, S1, S2, S3, cast, trans, evac]

        def outproj_ln(layer, s, last):
            w0, Rw0 = load_w(wo_h[layer, 0])
            w1, Rw1 = load_w(wo_h[layer, 1])
            P.handoff(qk_res, ln_res)
            P.dma("sp", lngain[:], lg_h[layer].broadcast_to([128, D]), writes=[Rlng])
            P.dma("sp", lnbias[:], lb_h[layer].broadcast_to([128, D]), writes=[Rlnb])
            items = []
            for tb in range(NB):
                items.append(ln_item(s, tb, last, w0, Rw0, w1, Rw1))
            n = len(items)
            order = [0, 1, 2, 3, 4, 6, 5]
            for i in range(n + 6):
                for sidx in order:
                    if 0 <= i - sidx < n:
                        items[i - sidx][sidx]()
            P.handoff(ln_res, qk_res)
            P.op("pool", ones_init, writes=[RVones])

        def main_program(stage):
            for s in range(n_seq):
                for tb in range(NB):
                    P.dma("sp", xres[:, tb, :], x_h[s, tb * 128:(tb + 1) * 128, :], writes=[Rxres[tb]])
                for tb in range(NB):
                    make_xT(tb)
                for layer in range(n_layers):
                    P.dma("sp", gng[:], gn_h[layer].broadcast_to([128, 384]), writes=[Rgng])
                    nxt = load_w(wp_h[layer, 0])
                    fox_prep(layer)
                    stage("fox_prep")
                    for hp in range(8):
                        wb, Rwb = nxt
                        if hp < 7:
                            nxt = load_w(wp_h[layer, hp + 1])
                        if hp < 3:
                            setup, qkv, gate = inproj_headwise(wb, Rwb, "fox")
                        elif hp < 6:
                            setup, qkv, gate = inproj_ret(wb, Rwb, hp - 3)
                        else:
                            setup, qkv, gate = inproj_headwise(wb, Rwb, "sb")
                        setup()
                        if INTERLEAVE:
                            for f in qkv[0] + gate:
                                f()
                        else:
                            for t in range(NT):
                                for f in qkv[t]:
                                    f()
                            for f in gate:
                                f()
                        if hp < 3:
                            for hi in range(2):
                                gh = 2 * hp + hi
                                P.dma("sp", Qh[hi][64:65, :], negcspT[gh:gh + 1, :], reads=Rncsp, writes=[RQaug[hi]] + RQ[1])
                        gstate = {}
                        if not INTERLEAVE and hp < 3:
                            items = []
                            for t in range(NT):
                                items += fox_items(0, 2 * hp, hp, t) + fox_items(1, 2 * hp + 1, hp, t)
                            pipeline(items, 2)
                        elif not INTERLEAVE and hp < 6:
                            pre, items = [], []
                            for t in range(NT):
                                st = ret_stream(hp - 3, hp, t, gstate, raw=True)
                                pre.append(st[0])
                                items += st[1]
                            for f in pre:
                                f()
                            pipeline(items, RET_DEPTH)
                            for ent in sorted(gstate.get("pend", []), key=lambda x: x[0]):
                                ent[1]()
                        if not INTERLEAVE and hp >= 6:
                            sb_pair(hp)
                        for t in range(NT):
                            if not INTERLEAVE:
                                break
                            if hp < 3:
                                main = flatten(fox_items(0, 2 * hp, hp, t) + fox_items(1, 2 * hp + 1, hp, t), 2)
                            elif hp < 6:
                                main = ret_stream(hp - 3, hp, t, gstate)
                            else:
                                main = sb_stream(0, hp, t) + sb_stream(1, hp, t)
                            run_merged(main, qkv[t + 1] if (INTERLEAVE and t + 1 < NT) else [])
                        stage("pair %d" % hp)
                    if dbg_y and s == 0 and layer == 0:
                        P.dma("sp", dbgy_h, yT[:].rearrange("p k t -> p (k t)"), reads=[RyT[c][t] for c in range(8) for t in range(NT)])
                    outproj_ln(layer, s, last=(layer == n_layers - 1))

        class StopBuild(Exception):
            pass
        stg = {"n": 0}

        def stage(name):
            stg["n"] += 1
            if dbg_stop is not None and stg["n"] == dbg_stop:
                print("debug stop at stage", stg["n"], name)
                raise StopBuild()

        try:
            main_program(stage)
        except StopBuild:
            for tb in range(NB):
                P.dma("sp", out_h[0, tb * 128:(tb + 1) * 128, :], xres[:, tb, :], reads=[Rxres[tb]])
        P.final_wait_all("sp")
        P.emit()
    return nc


def host_consts():
    k = np.arange(128)[:, None]
    q = np.arange(128)[None, :]
    ident = (k == q).astype(np.float32)
    m01 = (k <= q).astype(np.float32)
    m01s = (k < q).astype(np.float32)
    negfox = np.where(k > q, NEG, 0.0).astype(np.float32)
    negsb = np.where(k >= q, NEG, 0.0).astype(np.float32)
    neguinc = np.where(k >= q, -1.0, 0.0).astype(np.float32)
    negones = -np.ones((128, 128), np.float32)
    c128 = np.concatenate([ident, m01, m01s, negfox, negsb, neguinc, negones, np.zeros((128, 128), np.float32)], axis=1)
    oh = np.zeros((128, 32), np.float32)
    oh[:, 15] = 1.0
    perm = np.zeros((128, 128), np.float32)
    for p in range(128):
        perm[p + 32 if (p % 64) < 32 else p - 32, p] = 1.0
    msel = np.where(np.arange(16)[:, None] > (np.arange(S)[None, :] // 128), -1.0, 0.0).astype(np.float32)
    half = 32
    inv_freq = (1.0 / (10000.0 ** (np.arange(half, dtype=np.float32) / half))).astype(np.float32)
    pos = np.arange(S, dtype=np.float32)
    ang = pos[None, :] * inv_freq[:, None]
    cos32 = np.cos(ang).astype(np.float32)
    sin32 = np.sin(ang).astype(np.float32)
    cost = np.tile(cos32, (4, 1))
    sint = np.concatenate([-sin32, sin32, -sin32, sin32], axis=0)
    iota = np.tile(np.arange(512, dtype=np.float32)[None, :], (128, 1))
    log_g = np.log(1.0 - 2.0 ** (-5.0 - np.arange(6, dtype=np.float64)))
    dec = np.zeros((128, 30), np.float32)
    for rp in range(3):
        for p in range(128):
            lg = log_g[2 * rp + (1 if p >= 64 else 0)]
            dec[p, rp * 10 + 0] = lg
            dec[p, rp * 10 + 1] = -lg
            for t in range(4):
                dec[p, rp * 10 + 2 + t] = lg * 512 * t
                dec[p, rp * 10 + 6 + t] = -lg * 512 * t + math.log(0.125)
    return dict(c128=c128, oh=oh, msel=msel, perm=perm, cost=cost, sint=sint, iota=iota, dec=dec)


def host_weights(w_in, w_out, b_fgate, ret_gn_gain, ln_gain, ln_bias):
    L = w_in.shape[0]
    w4 = np.ascontiguousarray(w_in[:, :, :4096]).reshape(L, NKC, 128, 4, 8, 128)
    wp = np.ascontiguousarray(w4.transpose(0, 4, 2, 1, 3, 5)).reshape(L, 8, 128, NKC * 512)
    wf = np.ascontiguousarray(w_in[:, :, 4096:4102].reshape(L, NKC, 128, 6).transpose(0, 2, 1, 3)).reshape(L, 128, NKC * 6)
    wo4 = w_out.reshape(L, NKC, 128, 2, 512)
    wo = np.ascontiguousarray(wo4.transpose(0, 3, 2, 1, 4)).reshape(L, 2, 128, NKC * 512)
    return dict(wp=wp, wf=wf, wo=wo,
                bfg=np.ascontiguousarray(b_fgate.reshape(L, 6, 1)),
                gng=np.ascontiguousarray(ret_gn_gain.reshape(L, 1, 384)),
                lng=np.ascontiguousarray(ln_gain.reshape(L, 1, D)),
                lnb=np.ascontiguousarray(ln_bias.reshape(L, 1, D)))


_NC_CACHE = {}


def kernel(x, w_in, b_fgate, ret_gn_gain, w_out, ln_gain, ln_bias):
    x = np.asarray(x, np.float32)
    n_cores = 8
    n_seq = x.shape[0] // n_cores
    shared = host_consts()
    shared.update(host_weights(np.asarray(w_in, np.float32), np.asarray(w_out, np.float32), np.asarray(b_fgate, np.float32),
                               np.asarray(ret_gn_gain, np.float32), np.asarray(ln_gain, np.float32), np.asarray(ln_bias, np.float32)))
    if "nc" not in _NC_CACHE:
        _NC_CACHE["nc"] = build(n_seq, DEPTH)
    nc = _NC_CACHE["nc"]
    in_maps = []
    for c in range(n_cores):
        m = dict(shared)
        m["x"] = np.ascontiguousarray(x[c * n_seq:(c + 1) * n_seq])
        in_maps.append(m)
    res = run_bass_kernel_spmd(nc, in_maps, core_ids=list(range(n_cores)))
    return np.concatenate([r["out"] for r in res.results], axis=0).astype(np.float32)
```
